# Optimizing a Trainium2 kernel written in Bass

```python
import functools
import jax
import jax.numpy as jnp
from jax import lax
import numpy as np

D_MODEL = 1024
BATCH = 32
SEQ = 2048
DEPTH = 4
DEC_BATCH = 8
DEC_SEQ = 32
PAST_LEN = 1024

CHUNK = 64
WINDOW = 128
WIN_CHUNKS = WINDOW // CHUNK
A_HEADS = 8
A_KV_HEADS = 2
A_HEAD_DIM = 64
A_GROUP = A_HEADS // A_KV_HEADS
A_SCALE = A_HEAD_DIM ** -0.5
ROT_DIM = A_HEAD_DIM // 4
ROPE_THETA = 500000.0
B_HEADS = 4
B_DK = (D_MODEL // 2) // B_HEADS
B_DV = D_MODEL // B_HEADS
GATE_RANK = 16
GATE_TAU = 16.0
N_MEM = 256
M_HEADS = 4
M_HEAD_DIM = D_MODEL // 8
D_FF = 4 * D_MODEL
N_BRANCH = 3
DN_ALPHA = (2 * DEPTH) ** 0.25
DN_BETA = (8 * DEPTH) ** -0.25
LN_EPS = 1e-5
RMS_EPS = 1e-6
NEG_INF = -1e30

A_Q = A_HEADS * A_HEAD_DIM
A_KV = A_KV_HEADS * A_HEAD_DIM
B_QK = B_HEADS * B_DK
B_V = B_HEADS * B_DV
M_Q = M_HEADS * M_HEAD_DIM
SPLITS = (A_Q, A_KV, A_KV, B_QK, B_QK, B_V, B_V, GATE_RANK, M_Q, N_BRANCH * D_MODEL)
D_IN = A_Q + 2 * A_KV + 2 * B_QK + 2 * B_V + GATE_RANK + M_Q + N_BRANCH * D_MODEL

kernel_name = 'hybrid_streaming_encoder_step'


def split_columns(h):
    parts, off = [], 0
    for width in SPLITS:
        parts.append(h[..., off:off + width])
        off += width
    return parts


def layer_norm(x, g, b):
    xf = x.astype(jnp.float32)
    mu = jnp.mean(xf, axis=-1, keepdims=True)
    var = jnp.mean(jnp.square(xf - mu), axis=-1, keepdims=True)
    return ((xf - mu) * lax.rsqrt(var + LN_EPS) * g.astype(jnp.float32) + b.astype(jnp.float32)).astype(x.dtype)


def rms_norm(x, g):
    xf = x.astype(jnp.float32)
    return xf * lax.rsqrt(jnp.mean(xf * xf, axis=-1, keepdims=True) + RMS_EPS) * g.astype(jnp.float32)


def rope(x, pos):
    half = ROT_DIM // 2
    inv = ROPE_THETA ** (-jnp.arange(half, dtype=jnp.float32) / half)
    ang = pos.astype(jnp.float32)[:, None] * inv[None, :]
    cos = jnp.cos(ang)[:, None, :].astype(x.dtype)
    sin = jnp.sin(ang)[:, None, :].astype(x.dtype)
    x1 = x[..., :half]
    x2 = x[..., half:ROT_DIM]
    return jnp.concatenate([x1 * cos - x2 * sin, x2 * cos + x1 * sin, x[..., ROT_DIM:]], axis=-1)


def sink_softmax(s, sink):
    sk = sink.astype(jnp.float32).reshape(A_KV_HEADS, A_GROUP)[:, :, None, None]
    m = jnp.maximum(jnp.max(s, axis=-1, keepdims=True), sk)
    p = jnp.exp(s - m)
    return p / (jnp.sum(p, axis=-1, keepdims=True) + jnp.exp(sk - m))


def window_attn_prompt(q, k, v, sink):
    B, S = q.shape[0], q.shape[1]
    n = S // CHUNK
    J = (WIN_CHUNKS + 1) * CHUNK
    qb = q.reshape(B, n, CHUNK, A_KV_HEADS, A_GROUP, A_HEAD_DIM)
    pad = ((0, 0), (WIN_CHUNKS, 0), (0, 0), (0, 0), (0, 0))
    kb = jnp.pad(k.reshape(B, n, CHUNK, A_KV_HEADS, A_HEAD_DIM), pad)
    vb = jnp.pad(v.reshape(B, n, CHUNK, A_KV_HEADS, A_HEAD_DIM), pad)
    kw = jnp.concatenate([kb[:, j:j + n] for j in range(WIN_CHUNKS + 1)], axis=2)
    vw = jnp.concatenate([vb[:, j:j + n] for j in range(WIN_CHUNKS + 1)], axis=2)
    s = jnp.einsum('bnqkgd,bnjkd->bnkgqj', qb, kw).astype(jnp.float32) * A_SCALE
    key_chunk = jnp.arange(n)[:, None] - WIN_CHUNKS + jnp.arange(J)[None, :] // CHUNK
    s = jnp.where((key_chunk >= 0)[None, :, None, None, None, :], s, NEG_INF)
    p = sink_softmax(s, sink).astype(v.dtype)
    o = jnp.einsum('bnkgqj,bnjkd->bnqkgd', p, vw).reshape(B, S, A_Q)
    keep = min(WINDOW, S)
    return o, k[:, S - keep:], v[:, S - keep:]


def window_attn_sample(q, k, v, sink, k_cache, v_cache):
    B, T = q.shape[0], q.shape[1]
    kk = jnp.concatenate([k_cache.astype(k.dtype), k], axis=1)
    vv = jnp.concatenate([v_cache.astype(v.dtype), v], axis=1)
    qg = q.reshape(B, T, A_KV_HEADS, A_GROUP, A_HEAD_DIM)
    s = jnp.einsum('btkgd,bjkd->bkgtj', qg, kk).astype(jnp.float32) * A_SCALE
    p = sink_softmax(s, sink).astype(vv.dtype)
    o = jnp.einsum('bkgtj,bjkd->btkgd', p, vv).reshape(B, T, A_Q)
    keep = k_cache.shape[1]
    return o, kk[:, -keep:], vv[:, -keep:]


def gla_scan(q, k, v, log_a, s0, block):
    f32 = jnp.float32
    B, S, H = q.shape[0], q.shape[1], q.shape[2]
    n = S // block
    qf = q.astype(f32).reshape(B, n, block, H, B_DK) * (B_DK ** -0.5)
    kf = k.astype(f32).reshape(B, n, block, H, B_DK)
    vf = v.astype(f32).reshape(B, n, block, H, B_DV)
    cum = jnp.cumsum(log_a.reshape(B, n, block, H, B_DK), axis=2)
    last = cum[:, :, -1]
    q_dec = qf * jnp.exp(cum)
    k_inv = kf * jnp.exp(-cum)
    k_end = kf * jnp.exp(last[:, :, None] - cum)
    causal = jnp.tril(jnp.ones((block, block), f32))
    att = jnp.einsum('bnthd,bnshd->bnhts', q_dec, k_inv) * causal
    o_intra = jnp.einsum('bnhts,bnshv->bnthv', att, vf)

    def step(state, xs):
        qd, ke, vb, lb = xs
        o = jnp.einsum('bthd,bhdv->bthv', qd, state)
        state = jnp.exp(lb)[..., None] * state + jnp.einsum('bthd,bthv->bhdv', ke, vb)
        return state, o

    xs = (jnp.moveaxis(q_dec, 1, 0), jnp.moveaxis(k_end, 1, 0), jnp.moveaxis(vf, 1, 0), jnp.moveaxis(last, 1, 0))
    s_fin, o_inter = lax.scan(step, s0.astype(f32), xs)
    o = o_intra + jnp.moveaxis(o_inter, 0, 1)
    return o.reshape(B, S, H, B_DV), s_fin


def memory_attn(q, mem_k, mem_v):
    B, S = q.shape[0], q.shape[1]
    s = jnp.einsum('bshd,bmhd->bhsm', q, mem_k.astype(q.dtype)).astype(jnp.float32) * (M_HEAD_DIM ** -0.5)
    p = jax.nn.softmax(s, axis=-1).astype(q.dtype)
    return jnp.einsum('bhsm,bmhd->bshd', p, mem_v.astype(q.dtype)).reshape(B, S, M_Q)


def setup_inputs(seed: int = 0) -> dict:
    key = jax.random.key(seed)
    ks = jax.random.split(key, 32)
    f32 = jnp.float32

    def nrm(k, shape, scale):
        return jax.random.normal(k, shape, f32) * scale

    win_keep = min(WINDOW, PAST_LEN)
    return {
        'x_prompt': nrm(ks[0], (BATCH, SEQ, D_MODEL), 1.0),
        'x_sample': nrm(ks[1], (DEC_BATCH, DEC_SEQ, D_MODEL), 1.0),
        'cache_win_k': nrm(ks[2], (DEPTH, DEC_BATCH, win_keep, A_KV_HEADS, A_HEAD_DIM), 1.0),
        'cache_win_v': nrm(ks[3], (DEPTH, DEC_BATCH, win_keep, A_KV_HEADS, A_HEAD_DIM), 1.0),
        'state_gla': nrm(ks[4], (DEPTH, DEC_BATCH, B_HEADS, B_DK, B_DV), 1.0),
        'cache_mem_k': nrm(ks[5], (DEPTH, DEC_BATCH, N_MEM, M_HEADS, M_HEAD_DIM), 1.0),
        'cache_mem_v': nrm(ks[6], (DEPTH, DEC_BATCH, N_MEM, M_HEADS, M_HEAD_DIM), 1.0),
        'mem_prompt': nrm(ks[7], (BATCH, N_MEM, D_MODEL), 1.0),
        'w_in': nrm(ks[8], (DEPTH, D_MODEL, D_IN), D_MODEL ** -0.5),
        'w_gk2': nrm(ks[9], (DEPTH, GATE_RANK, B_QK), GATE_RANK ** -0.5),
        'b_gk': nrm(ks[10], (DEPTH, B_QK), 0.1),
        'attn_sinks': nrm(ks[11], (DEPTH, A_HEADS), 0.5),
        'gla_norm_g': 1.0 + nrm(ks[12], (DEPTH, B_DV), 0.02),
        'w_mem_kv': nrm(ks[13], (DEPTH, D_MODEL, 2 * M_Q), D_MODEL ** -0.5),
        'w_proj_a': nrm(ks[14], (DEPTH, A_Q, D_MODEL), A_Q ** -0.5),
        'w_proj_b': nrm(ks[15], (DEPTH, B_V, D_MODEL), B_V ** -0.5),
        'w_proj_m': nrm(ks[16], (DEPTH, M_Q, D_MODEL), M_Q ** -0.5),
        'w_out': nrm(ks[17], (DEPTH, D_MODEL, D_MODEL), D_MODEL ** -0.5 * DN_BETA),
        'ln1_g': 1.0 + nrm(ks[18], (DEPTH, D_MODEL), 0.02),
        'ln1_b': nrm(ks[19], (DEPTH, D_MODEL), 0.01),
        'w_up': nrm(ks[20], (DEPTH, D_MODEL, D_FF), D_MODEL ** -0.5),
        'b_up': nrm(ks[21], (DEPTH, D_FF), 0.01),
        'w_down': nrm(ks[22], (DEPTH, D_FF, D_MODEL), D_FF ** -0.5 * DN_BETA),
        'b_down': nrm(ks[23], (DEPTH, D_MODEL), 0.01),
        'ln2_g': 1.0 + nrm(ks[24], (DEPTH, D_MODEL), 0.02),
        'ln2_b': nrm(ks[25], (DEPTH, D_MODEL), 0.01),
    }


def reference(x_prompt, x_sample, cache_win_k, cache_win_v, state_gla, cache_mem_k, cache_mem_v, mem_prompt,
              w_in, w_gk2, b_gk, attn_sinks, gla_norm_g, w_mem_kv, w_proj_a, w_proj_b, w_proj_m, w_out,
              ln1_g, ln1_b, w_up, b_up, w_down, b_down, ln2_g, ln2_b):
    f32 = jnp.float32

    def layer(x, pos, l, window_fn, gla_s0, gla_block, mem_k, mem_v):
        B, S = x.shape[0], x.shape[1]
        aq, ak, av, bq, bk, bv, bg, bgk, mq, gates = split_columns(x @ w_in[l])
        aq = rope(aq.reshape(B, S, A_HEADS, A_HEAD_DIM), pos)
        ak = rope(ak.reshape(B, S, A_KV_HEADS, A_HEAD_DIM), pos)
        o_a, k_keep, v_keep = window_fn(aq, ak, av.reshape(B, S, A_KV_HEADS, A_HEAD_DIM), attn_sinks[l])
        log_a = jax.nn.log_sigmoid((bgk @ w_gk2[l] + b_gk[l]).astype(f32)) / GATE_TAU
        o_b, s_new = gla_scan(bq.reshape(B, S, B_HEADS, B_DK), bk.reshape(B, S, B_HEADS, B_DK),
                              bv.reshape(B, S, B_HEADS, B_DV), log_a.reshape(B, S, B_HEADS, B_DK),
                              gla_s0, gla_block)
        o_b = (rms_norm(o_b, gla_norm_g[l]) * jax.nn.silu(bg.reshape(B, S, B_HEADS, B_DV).astype(f32))).astype(x.dtype)
        o_m = memory_attn(mq.reshape(B, S, M_HEADS, M_HEAD_DIM), mem_k, mem_v)
        g_a, g_b, g_m = jnp.split(jax.nn.sigmoid(gates), N_BRANCH, axis=-1)
        merged = (g_a * (o_a @ w_proj_a[l])
                  + g_b * (o_b.reshape(B, S, B_V) @ w_proj_b[l])
                  + g_m * (o_m @ w_proj_m[l]))
        x = layer_norm(DN_ALPHA * x + merged @ w_out[l], ln1_g[l], ln1_b[l])
        ff = jnp.square(jax.nn.relu(x @ w_up[l] + b_up[l])) @ w_down[l] + b_down[l]
        x = layer_norm(DN_ALPHA * x + ff, ln2_g[l], ln2_b[l])
        return x, k_keep, v_keep, s_new.astype(x.dtype)

    Bp, Sp = x_prompt.shape[0], x_prompt.shape[1]
    pos_p = jnp.arange(Sp, dtype=jnp.int32)
    x = x_prompt
    wk_p, wv_p, gs_p, mk_p, mv_p = [], [], [], [], []
    for l in range(DEPTH):
        mkv = (mem_prompt.astype(x.dtype) @ w_mem_kv[l]).reshape(Bp, N_MEM, 2, M_HEADS, M_HEAD_DIM)
        mk, mv = mkv[:, :, 0], mkv[:, :, 1]
        s0 = jnp.zeros((Bp, B_HEADS, B_DK, B_DV), f32)
        x, kk, vv, st = layer(x, pos_p, l, window_attn_prompt, s0, CHUNK, mk, mv)
        wk_p.append(kk)
        wv_p.append(vv)
        gs_p.append(st)
        mk_p.append(mk)
        mv_p.append(mv)
    y_prompt = x

    T = x_sample.shape[1]
    pos_s = PAST_LEN + jnp.arange(T, dtype=jnp.int32)
    x = x_sample
    wk_s, wv_s, gs_s = [], [], []
    for l in range(DEPTH):
        fn = functools.partial(window_attn_sample, k_cache=cache_win_k[l], v_cache=cache_win_v[l])
        x, kk, vv, st = layer(x, pos_s, l, fn, state_gla[l], T, cache_mem_k[l], cache_mem_v[l])
        wk_s.append(kk)
        wv_s.append(vv)
        gs_s.append(st)
    y_sample = x

    return (y_prompt, y_sample, jnp.stack(wk_p), jnp.stack(wv_p), jnp.stack(gs_p), jnp.stack(mk_p),
            jnp.stack(mv_p), jnp.stack(wk_s), jnp.stack(wv_s), jnp.stack(gs_s))
```

```python
import numpy as np
from contextlib import ExitStack
import concourse.bass as bass
import concourse.mybir as mybir
from concourse.bass_utils import run_bass_kernel_spmd

F32 = mybir.dt.float32
BF16 = mybir.dt.bfloat16
AF = mybir.ActivationFunctionType
ALU = mybir.AluOpType
ENGS = ("tensor", "vector", "scalar", "gpsimd", "sync")
NDMASEM = 20

D = 1024
NL = 4
SEQ = 2048
NB = 32
TS = 32
PAST = 1024
DIN = 7440
C_AQ, C_AK, C_AV, C_BQ, C_BK, C_BV, C_BG, C_BGK, C_MQ, C_G = 0, 512, 640, 768, 1280, 1792, 2816, 3840, 3856, 4368
ALPHA = float(8 ** 0.25)
A_SCALE = 64 ** -0.5
M_SCALE = 128 ** -0.5
BQ_SCALE = 128 ** -0.5
LN_EPS = 1e-5
RMS_EPS = 1e-6
T = 512
RING = 8
ARENA = 7680


def _is_ap(x):
    return hasattr(x, "tensor") and hasattr(x, "ap") and hasattr(x, "offset")


class Prog:
    def __init__(self, nc):
        self.nc = nc
        self.ops = []
        self.ev = {}
        self.dma_cnt = {e: 0 for e in ENGS}
        self.dma_hist = {e: [] for e in ENGS}
        self.bank_i = 0

    def region(self, ap):
        t = ap.tensor
        if type(t).__name__.startswith("DRam"):
            return None
        dims = ap.ap
        esz = mybir.dt.size(ap.dtype)
        row = dims[0][0]
        npart = dims[0][1]
        off = ap.offset
        if row > 0:
            p0 = off // row
            f0 = off % row
        else:
            p0 = 0
            f0 = off
        ext = 1
        for s, c in dims[1:]:
            ext += (c - 1) * abs(s)
        return (t.name, p0, p0 + npart, f0 * esz, (f0 + ext) * esz)

    def _access(self, idx, eng, is_dma, regs, is_write, deps):
        for r in regs:
            if r is None:
                continue
            name, plo, phi, blo, bhi = r
            lst = self.ev.get(name, [])
            keep = []
            for e in lst:
                ov = (e[0] < phi and plo < e[1] and e[2] < bhi and blo < e[3])
                if ov and (is_write or e[5]):
                    if e[4] != idx:
                        deps.add(e[4])
                if is_write and ov and e[0] >= plo and e[1] <= phi and e[2] >= blo and e[3] <= bhi and e[4] != idx:
                    continue
                keep.append(e)
            key = None if (is_write or is_dma) else (eng, plo, phi, blo, bhi)
            if key is not None:
                keep = [e for e in keep if e[6] != key]
            keep.append([plo, phi, blo, bhi, idx, is_write, key])
            self.ev[name] = keep

    def op(self, eng, fn, reads=(), writes=(), dma=False):
        idx = len(self.ops)
        deps = set()
        rr = [self.region(a) if _is_ap(a) else (a, 0, 128, 0, 1 << 40) for a in reads if a is not None]
        ww = [self.region(a) if _is_ap(a) else (a, 0, 128, 0, 1 << 40) for a in writes if a is not None]
        pr = []
        for lst in (rr, ww):
            for r in list(lst):
                if r is not None and r[0] == "ps":
                    lst.remove(r)
                    b0, b1 = r[3] // 2048, (r[4] - 1) // 2048
                    pr.append(("ps", 0, 128, b0 * 2048, (b1 + 1) * 2048))
        self._access(idx, eng, dma, rr, False, deps)
        self._access(idx, eng, dma, ww, True, deps)
        self._access(idx, eng, dma, pr, True, deps)
        o = dict(eng=eng, fn=fn, deps=deps, dma=dma, sig=False)
        if dma:
            n = self.dma_cnt[eng]
            self.dma_cnt[eng] = n + 1
            hist = self.dma_hist[eng]
            if n >= NDMASEM:
                deps.add(hist[n - NDMASEM])
            hist.append(idx)
            o["semval"] = ("d", (eng, n % NDMASEM), 16 * (n // NDMASEM + 1))
        self.ops.append(o)
        return idx

    def bank(self):
        b = self.bank_i
        self.bank_i = (b + 1) % 6
        return b

    def emit(self, stack):
        nc = self.nc
        ops = self.ops
        for o in ops:
            if o["eng"] == "tensor" and not o["dma"]:
                o["deps"] = {d for d in o["deps"] if not (ops[d]["eng"] == "tensor" and not ops[d]["dma"])}
        for o in ops:
            for d in o["deps"]:
                if not ops[d]["dma"]:
                    ops[d]["sig"] = True
        eng_cnt = {e: 0 for e in ENGS}
        for o in ops:
            if (not o["dma"]) and o["sig"]:
                eng_cnt[o["eng"]] += 1
                o["semval"] = ("e", o["eng"], eng_cnt[o["eng"]])
        esem = {e: stack.enter_context(nc.semaphore("es_" + e)) for e in ENGS}
        dsem = {}
        dfinal = {}
        for e in ENGS:
            for i in range(min(NDMASEM, self.dma_cnt[e])):
                dsem[(e, i)] = stack.enter_context(nc.semaphore("ds_%s_%d" % (e, i)))
        for o in ops:
            if o["dma"]:
                _, k, v = o["semval"]
                dfinal[k] = max(dfinal.get(k, 0), v)
        per_eng = {e: [] for e in ENGS}
        for o in ops:
            per_eng[o["eng"]].append(o)
        block = stack.enter_context(nc.Block())

        def make(ename):
            def body(eng):
                waited = {}
                for o in per_eng[ename]:
                    need = {}
                    for d in o["deps"]:
                        kind, key, val = ops[d]["semval"]
                        kk = (kind, key)
                        if waited.get(kk, 0) >= val:
                            continue
                        if need.get(kk, 0) < val:
                            need[kk] = val
                    for kk, val in need.items():
                        sem = esem[kk[1]] if kk[0] == "e" else dsem[kk[1]]
                        eng.wait_ge(sem, val)
                        waited[kk] = val
                    ins = o["fn"](eng)
                    if o["dma"]:
                        ins.then_inc(dsem[o["semval"][1]], 16)
                    elif o["sig"]:
                        ins.then_inc(esem[ename], 1)
                if ename == "sync":
                    for k, v in dfinal.items():
                        eng.wait_ge(dsem[k], v)
                    for e in ENGS:
                        if eng_cnt[e]:
                            eng.wait_ge(esem[e], eng_cnt[e])
            return body

        block.tensor(make("tensor"))
        block.vector(make("vector"))
        block.scalar(make("scalar"))
        block.gpsimd(make("gpsimd"))
        block.sync(make("sync"))

    def mm(self, out, lhsT, rhs, start=True, stop=True):
        self.op("tensor", lambda e: e.matmul(out, lhsT=lhsT, rhs=rhs, start=start, stop=stop),
                reads=[lhsT, rhs], writes=[out])

    def tr(self, out, in_, ident):
        self.op("tensor", lambda e: e.transpose(out, in_, ident), reads=[in_, ident], writes=[out])

    def act(self, out, in_, func, bias=None, scale=None):
        kw = {}
        rd = [in_]
        if bias is not None:
            kw["bias"] = bias
            if _is_ap(bias):
                rd.append(bias)
        if scale is not None:
            kw["scale"] = scale
            if _is_ap(scale):
                rd.append(scale)
        self.op("scalar", lambda e: e.activation(out=out, in_=in_, func=func, **kw), reads=rd, writes=[out])

    def tt(self, out, in0, in1, op, eng="vector"):
        self.op(eng, lambda e: e.tensor_tensor(out=out, in0=in0, in1=in1, op=op), reads=[in0, in1], writes=[out])

    def ts(self, out, in0, s1, op0, s2=None, op1=None, eng="vector"):
        rd = [in0] + [s for s in (s1, s2) if _is_ap(s)]
        if op1 is None:
            self.op(eng, lambda e: e.tensor_scalar(out=out, in0=in0, scalar1=s1, scalar2=None, op0=op0),
                    reads=rd, writes=[out])
        else:
            self.op(eng, lambda e: e.tensor_scalar(out=out, in0=in0, scalar1=s1, scalar2=s2, op0=op0, op1=op1),
                    reads=rd, writes=[out])

    def stt(self, out, in0, scalar, in1, op0, op1):
        rd = [in0, in1] + ([scalar] if _is_ap(scalar) else [])
        self.op("vector", lambda e: e.scalar_tensor_tensor(out=out, in0=in0, scalar=scalar, in1=in1, op0=op0, op1=op1),
                reads=rd, writes=[out])

    def copy(self, out, in_, eng="vector"):
        if eng == "scalar":
            self.act(out, in_, AF.Copy)
        else:
            self.op(eng, lambda e: e.tensor_copy(out=out, in_=in_), reads=[in_], writes=[out])

    def recip(self, out, in_):
        self.op("vector", lambda e: e.reciprocal(out=out, in_=in_), reads=[in_], writes=[out])

    def memset(self, out, val, eng="vector"):
        self.op(eng, lambda e: e.memset(out, val), writes=[out])

    def dma(self, q, out, in_, reads=(), writes=()):
        self.op(q, lambda e: e.dma_start(out=out, in_=in_), reads=[in_] + list(reads), writes=[out] + list(writes),
                dma=True)


def _consts():
    c = {}
    c["c_ident"] = np.eye(128, dtype=np.float32)
    s = np.arange(128)[:, None]
    t = np.arange(128)[None, :]
    tri = np.zeros((128, 2, 128), np.float32)
    tri[:, 0, :] = np.where(s <= t, -1.0 / 16.0, 0.0)
    tri[:, 1, :] = np.where(s > t, -1.0 / 16.0, 0.0)
    c["c_tri"] = tri
    c["c_mask"] = np.where(s <= t, 1.0, 0.0).astype(np.float32)
    cb = np.zeros((128, 5, 128), np.float32)
    cb[:, 0, :] = np.eye(128)
    cb[:, 1, :] = 1.0
    cb[:, 2, 0:64] = 1.0
    cb[:, 3, 64:128] = 1.0
    cb[:, 4, :] = 1.0 / 1024.0
    c["c_bf"] = cb
    row = np.zeros((2, 1408), np.float32)
    row[0, 0:64] = 1.0
    row[1, 64:128] = 1.0
    row[0, 128:192] = 1.0
    row[0, 256 + 64:256 + 128] = 1.0
    q = np.arange(512) % 128
    row[0, 384:896] = np.where(q >= 64, -30000.0, 0.0)
    row[0, 896:1408] = np.where(q < 64, -30000.0, 0.0)
    c["c_row"] = row
    half = 8
    inv = (np.float32(500000.0) ** (-np.arange(half, dtype=np.float32) / np.float32(half))).astype(np.float32)
    rope = np.zeros((128, 17, 16), np.float32)
    for tile in range(17):
        if tile < 16:
            pos = (tile * 128 + np.arange(128)).astype(np.float32)
        else:
            pos = (PAST + np.arange(128)).astype(np.float32)
        ang = (pos[:, None] * inv[None, :]).astype(np.float32)
        rope[:, tile, 0:8] = np.cos(ang)
        rope[:, tile, 8:16] = np.sin(ang)
    c["c_rope"] = rope
    return c


class Builder:
    def __init__(self, n_seq=4, n_layers=NL, do_sample=True, n_st=4, dbg=False):
        self.n_seq = n_seq
        self.n_layers = n_layers
        self.do_sample = do_sample
        self.n_st = n_st
        self.dbg = dbg
        self.dbg_outs = {}

    def declare(self, nc):
        di = {}

        def I(name, shape):
            di[name] = nc.dram_tensor(name, list(shape), F32, kind="ExternalInput").ap()

        def O(name, shape):
            di[name] = nc.dram_tensor(name, list(shape), F32, kind="ExternalOutput").ap()

        ns = self.n_seq
        I("xp", (ns, SEQ, D)); I("xs", (TS, D)); I("cwk", (NL, 128, 128)); I("cwv", (NL, 128, 128))
        I("sg", (NL, 4, 128, 256)); I("cmk", (NL, 256, 512)); I("cmv", (NL, 256, 512)); I("memp", (ns, 256, D))
        I("w_in", (NL, D, DIN)); I("w_mem_kv", (NL, D, D)); I("w_proj_a", (NL, 512, D)); I("w_proj_b", (NL, D, D))
        I("w_proj_m", (NL, 512, D)); I("w_out", (NL, D, D)); I("w_up", (NL, D, 4096)); I("w_down", (NL, 4096, D))
        I("c_ident", (128, 128)); I("c_tri", (128, 2, 128)); I("c_mask", (128, 128)); I("c_bf", (128, 5, 128))
        I("c_row", (2, 1408)); I("c_rope", (128, 17, 16))
        I("lnp", (128, NL, 5, 8)); I("bup", (128, NL, 32)); I("gng", (128, NL, 2)); I("wgk2aug", (17, NL, 512))
        I("sinks", (2, NL, 4))
        O("yp", (ns, SEQ, D)); O("ys", (TS, D)); O("wkp", (NL, ns, 128, 128)); O("wvp", (NL, ns, 128, 128))
        O("gsp", (NL, ns, 4, 128, 256)); O("mkp", (NL, ns, 256, 512)); O("mvp", (NL, ns, 256, 512))
        O("wks", (NL, 128, 128)); O("wvs", (NL, 128, 128)); O("gss", (NL, 4, 128, 256))
        self.d = di
        self.wbf = nc.dram_tensor("wbf_scratch", [46 * NL, 128, 4096], BF16, kind="Internal").ap()
        self.wcache = {}

    def dbg_dump(self, name, ap, shape):
        if not self.dbg:
            return
        nc, P = self.nc, self.P
        o = nc.dram_tensor(name, list(shape), F32, kind="ExternalOutput").ap()
        self.dbg_outs[name] = tuple(shape)
        if ap.dtype == F32:
            P.dma("sync", o, ap)
        else:
            n = 1
            for s in shape[1:]:
                n *= s
            stg = self.dbgstage[0:shape[0], 0:n]
            if len(shape) == 3:
                stg = stg.rearrange("p (a b) -> p a b", b=shape[2])
            P.copy(stg, ap)
            P.dma("sync", o, stg)

    def dbg_dump_bf(self, name, t, C, Tn):
        if not self.dbg:
            return
        nc, P = self.nc, self.P
        o = nc.dram_tensor(name, [128, C, Tn], F32, kind="ExternalOutput").ap()
        for c in range(C):
            stg = self.dbgstage[:, (c % 2) * 512:(c % 2) * 512 + Tn]
            P.copy(stg, t[:, c, 0:Tn])
            P.dma("sync", o[:, c, :], stg)

    def build(self):
        nc = bass.Bass("TRN2", target_bir_lowering=False)
        self.nc = nc
        self.declare(nc)
        d = self.d
        with ExitStack() as st:
            def sb(name, shape, dt):
                return st.enter_context(nc.sbuf_tensor("s_" + name, list(shape), dt))

            P = Prog(nc)
            self.P = P
            self.ps = st.enter_context(nc.psum_tensor("ps", [128, 8, 512], F32))
            self.xres = sb("xres", (128, 8, T), F32)
            self.xb = sb("xb", (128, 8, T), BF16)
            self.oa = sb("oa", (128, 4, T), BF16)
            self.ob = sb("ob", (128, 8, T), BF16)
            self.om = sb("om", (128, 4, T), BF16)
            self.mg = sb("mg", (128, 8, T), BF16)
            self.ring_n = RING - (1 if self.dbg else 0)
            self.ring = sb("ring", (128, self.ring_n, 4096), BF16)
            self.ring_i = 0
            self.memT = sb("memT", (128, 8, 256), BF16)
            self.memkT = sb("memkT", (128, 4, 256), BF16)
            self.memv = sb("memv", (128, 2, 512), BF16)
            self.KTwin = sb("KTwin", (128, NL, 3, 128), BF16)
            self.Vwin = sb("Vwin", (128, NL, 3, 2, 128), BF16)
            self.state = sb("state", (128, NL, 4, 256), F32)
            self.state_bf = sb("state_bf", (128, 4, 256), BF16)
            self.xstage = sb("xstage", (128, 2, 1024), F32)
            self.ostage = sb("ostage", (128, 2, 512), F32)
            self.arena = sb("arena", (128, ARENA), F32)
            self.arena_bf = self.arena[:].bitcast(BF16)
            if self.dbg:
                self.dbgstage = sb("dbgstage", (128, 1024), F32)
            self.identf = sb("identf", (128, 128), F32)
            self.tri = sb("tri", (128, 2, 128), F32)
            self.cmask = sb("cmask", (128, 128), F32)
            self.cbf = sb("cbf", (128, 5, 128), BF16)
            self.crow = sb("crow", (2, 1408), BF16)
            self.rope = sb("rope", (128, 17, 16), F32)
            self.lnp = sb("lnp", (128, NL, 5, 8), F32)
            self.bup = sb("bup", (128, NL, 32), F32)
            self.gng = sb("gng", (128, NL, 2), F32)
            self.wgk = sb("wgk", (17, NL, 512), BF16)
            self.sinkf = sb("sinkf", (2, NL, 4), F32)
            self.sinke = sb("sinke", (2, NL, 4), F32)
            self.sinkrow = sb("sinkrow", (2, NL, 4, 128), BF16)
            self.bgkT = sb("bgkT", (17, 512), BF16)

            P.dma("sync", self.identf[:], d["c_ident"])
            P.dma("sync", self.tri[:], d["c_tri"])
            P.dma("sync", self.cmask[:], d["c_mask"])
            P.dma("gpsimd", self.cbf[:], d["c_bf"])
            P.dma("gpsimd", self.crow[:], d["c_row"])
            P.dma("sync", self.rope[:], d["c_rope"])
            P.dma("sync", self.lnp[:], d["lnp"])
            P.dma("sync", self.bup[:], d["bup"])
            P.dma("sync", self.gng[:], d["gng"])
            P.dma("gpsimd", self.wgk[:], d["wgk2aug"])
            P.dma("sync", self.sinkf[:], d["sinks"])
            skip = getattr(self, "skip", "").split(",")
            if "sink" not in skip:
                P.act(self.sinke[:], self.sinkf[:], AF.Exp)
                P.copy(self.sinkrow[:], self.sinke[:].unsqueeze(3).to_broadcast([2, NL, 4, 128]))
            if "memset" not in skip:
                P.memset(self.Vwin[:], 0.0)
                P.memset(self.bgkT[:], 1.0)
            self.identb = self.cbf[:, 0, :]
            self.ones = self.cbf[:, 1, :]
            self.ones_lo = self.cbf[:, 2, :]
            self.ones_hi = self.cbf[:, 3, :]
            self.onesm = self.cbf[:, 4, :]

            for b in range(self.n_seq):
                if "loadmem" not in skip:
                    self.load_mem(b)
                for sti in range(self.n_st):
                    ctx = dict(kind="p", b=b, st=sti, T=T, TA=128, NA=4)
                    if "loadx" not in skip:
                        self.load_x(ctx)
                    self.run_layers(ctx)
                    if "storey" not in skip:
                        self.store_y(ctx)
            if self.do_sample:
                ctx = dict(kind="s", b=0, st=0, T=TS, TA=TS, NA=1)
                self.load_x(ctx)
                self.run_layers(ctx)
                self.store_y(ctx)
            P.emit(st)
        return nc

    def arena_reset(self):
        self.acur = 0

    def af32(self, n):
        o = self.acur // 4
        self.acur += 4 * n
        assert self.acur <= 4 * ARENA, self.acur
        return self.arena[:, o:o + n]

    def abf(self, n):
        o = self.acur // 2
        self.acur += 2 * n
        self.acur = (self.acur + 3) // 4 * 4
        assert self.acur <= 4 * ARENA, self.acur
        return self.arena_bf[:, o:o + n]

    def pbank(self):
        return self.ps[:, self.P.bank(), :]

    def chunk(self, key, pieces):
        slot = self.ring_i % self.ring_n
        self.ring_i += 1
        ext = max(off + kc * n for (off, kc, n, _, _, _) in pieces)
        if key in self.wcache:
            ci = self.wcache[key]
            self.P.dma("gpsimd", self.ring[:, slot, 0:ext], self.wbf[ci, :, 0:ext], reads=["wbf%d" % ci])
        else:
            ci = len(self.wcache)
            self.wcache[key] = ci
            for (off, kc, n, src, p0, p1) in pieces:
                dst = self.ring[p0:p1, slot, off:off + kc * n].rearrange("p (k n) -> p k n", n=n)
                self.P.dma("gpsimd", dst, src)
            self.P.dma("sync", self.wbf[ci, :, 0:ext], self.ring[:, slot, 0:ext], writes=["wbf%d" % ci])
        return self.ring[:, slot, :]

    def wchunk(self, w, l, c0, n, r0=0, kc=8):
        src = w[l, r0:r0 + kc * 128, c0:c0 + n].rearrange("(k p) n -> p k n", p=128)
        v = self.chunk((w.tensor.name, l, c0, n, r0, kc), [(0, kc, n, src, 0, 128)])
        return v[:, 0:kc * n].rearrange("p (k n) -> p k n", n=n)

    def load_x(self, ctx):
        P, d = self.P, self.d
        TA, NA = ctx["TA"], ctx["NA"]
        for i in range(NA):
            slot = i % 2
            if ctx["kind"] == "p":
                t0 = ctx["st"] * T + i * TA
                src = d["xp"][ctx["b"], t0:t0 + TA, :]
            else:
                src = d["xs"][:, :]
            P.dma("sync", self.xstage[0:TA, slot, :], src)
            for hb in range(2):
                bk = self.pbank()
                for kk in range(4):
                    k = hb * 4 + kk
                    P.tr(bk[:, kk * 128:kk * 128 + TA], self.xstage[0:TA, slot, k * 128:(k + 1) * 128],
                         self.identf[0:TA, 0:TA])
                src_v = bk.rearrange("p (k t) -> p k t", t=128)[:, :, 0:TA]
                P.copy(self.xres[:, hb * 4:hb * 4 + 4, i * TA:(i + 1) * TA], src_v, eng="vector")
                P.copy(self.xb[:, hb * 4:hb * 4 + 4, i * TA:(i + 1) * TA], src_v, eng="scalar")

    def store_y(self, ctx):
        P, d = self.P, self.d
        TA, NA = ctx["TA"], ctx["NA"]
        for i in range(NA):
            slot = i % 2
            for hb in range(2):
                bk = self.pbank()
                for kk in range(4):
                    k = hb * 4 + kk
                    P.tr(bk[0:TA, kk * 128:(kk + 1) * 128], self.xres[:, k, i * TA:(i + 1) * TA], self.identf[:, :])
                if hb == 0:
                    P.copy(self.xstage[0:TA, slot, 0:512], bk[0:TA, :], eng="vector")
                else:
                    P.copy(self.xstage[0:TA, slot, 512:1024], bk[0:TA, :], eng="scalar")
            if ctx["kind"] == "p":
                t0 = ctx["st"] * T + i * TA
                dst = d["yp"][ctx["b"], t0:t0 + TA, :]
            else:
                dst = d["ys"][:, :]
            P.dma("sync", dst, self.xstage[0:TA, slot, :])

    def load_mem(self, b):
        P, d = self.P, self.d
        P.dma("sync", self.xstage[:, :, :], d["memp"][b].rearrange("(mt p) c -> p mt c", p=128))
        for mt in range(2):
            for hb in range(2):
                bk = self.pbank()
                for kk in range(4):
                    k = hb * 4 + kk
                    P.tr(bk[:, kk * 128:(kk + 1) * 128], self.xstage[:, mt, k * 128:(k + 1) * 128], self.identf[:, :])
                P.copy(self.memT[:, hb * 4:hb * 4 + 4, mt * 128:(mt + 1) * 128],
                       bk.rearrange("p (k t) -> p k t", t=128), eng=("vector" if hb == 0 else "scalar"))

    def run_layers(self, ctx):
        for l in range(self.n_layers):
            if l == 0:
                self.phase_mkv(l, ctx)
            self.phase_ab(l, ctx)
            self.phase_b2(l, ctx)
            self.phase_m(l, ctx)
            self.phase_merge(l, ctx)
            self.phase_wout(l, ctx)
            self.phase_mlp(l, ctx)
            if l + 1 < self.n_layers:
                self.phase_mkv(l + 1, ctx)
            self.arena_reset()
            self.ln_finish(l, 2, ctx)

    def phase_mkv(self, l, ctx):
        P, d = self.P, self.d
        if ctx["kind"] == "s":
            stg = self.xstage[:, 0, :].rearrange("p (mt c) -> p mt c", mt=2)
            P.dma("sync", stg, d["cmk"][l].rearrange("(mt p) c -> p mt c", p=128))
            for h in range(4):
                bk = self.pbank()
                for mt in range(2):
                    P.tr(bk[:, mt * 128:(mt + 1) * 128], stg[:, mt, h * 128:(h + 1) * 128], self.identf[:, :])
                P.copy(self.memkT[:, h, :], bk[:, 0:256], eng=("vector" if h % 2 == 0 else "scalar"))
            P.dma("gpsimd", self.memv[:, :, :], d["cmv"][l].rearrange("(mt p) c -> p mt c", p=128))
            return
        b, sti = ctx["b"], ctx["st"]
        cK = self.wchunk(d["w_mem_kv"], l, 0, 512)
        cV = self.wchunk(d["w_mem_kv"], l, 512, 512)
        first = (sti == 0)
        for mt in range(2):
            if first:
                bk = self.pbank()
                for k in range(8):
                    P.mm(bk, self.memT[:, k, mt * 128:(mt + 1) * 128], cK[:, k, :], start=(k == 0), stop=(k == 7))
                P.copy(self.ostage[:, 0, :], bk, eng="vector")
                P.dma("sync", d["mkp"][l, b, mt * 128:(mt + 1) * 128, :], self.ostage[:, 0, :])
            bk = self.pbank()
            for k in range(8):
                P.mm(bk, self.memT[:, k, mt * 128:(mt + 1) * 128], cV[:, k, :], start=(k == 0), stop=(k == 7))
            P.copy(self.memv[:, mt, :], bk, eng="scalar")
            if first:
                P.copy(self.ostage[:, 1, :], bk, eng="vector")
                P.dma("sync", d["mvp"][l, b, mt * 128:(mt + 1) * 128, :], self.ostage[:, 1, :])
        for h in range(4):
            bk = self.pbank()
            for k in range(8):
                P.mm(bk[:, 0:256], cK[:, k, h * 128:(h + 1) * 128], self.memT[:, k, :], start=(k == 0), stop=(k == 7))
            P.copy(self.memkT[:, h, :], bk[:, 0:256], eng=("vector" if h % 2 == 0 else "scalar"))

    def phase_m(self, l, ctx):
        P, d = self.P, self.d
        Tn = ctx["T"]
        cMq = self.wchunk(d["w_in"], l, C_MQ, 512)
        self.arena_reset()
        mqT = self.abf(4 * 512).rearrange("p (h t) -> p h t", h=4)
        PTs = [self.abf(2 * 512).rearrange("p (m t) -> p m t", m=2) for _ in range(2)]
        Rms = [self.af32(512) for _ in range(2)]
        for h in range(4):
            bk = self.pbank()
            for k in range(8):
                P.mm(bk[:, 0:Tn], cMq[:, k, h * 128:(h + 1) * 128], self.xb[:, k, 0:Tn], start=(k == 0), stop=(k == 7))
            P.copy(mqT[:, h, 0:Tn], bk[:, 0:Tn], eng=("vector" if h % 2 == 0 else "scalar"))
        def scores(h):
            PT = PTs[h % 2]
            for mt in range(2):
                bk = self.pbank()
                P.mm(bk[:, 0:Tn], self.memkT[:, h, mt * 128:(mt + 1) * 128], mqT[:, h, 0:Tn])
                P.act(PT[:, mt, 0:Tn], bk[:, 0:Tn], AF.Exp, scale=M_SCALE)

        scores(0)
        for h in range(4):
            PT = PTs[h % 2]
            Rm = Rms[h % 2]
            if h + 1 < 4:
                scores(h + 1)
            bD = self.pbank()
            P.mm(bD[:, 0:Tn], self.ones, PT[:, 0, 0:Tn], start=True, stop=False)
            P.mm(bD[:, 0:Tn], self.ones, PT[:, 1, 0:Tn], start=False, stop=True)
            bO = self.pbank()
            P.mm(bO[:, 0:Tn], self.memv[:, 0, h * 128:(h + 1) * 128], PT[:, 0, 0:Tn], start=True, stop=False)
            P.mm(bO[:, 0:Tn], self.memv[:, 1, h * 128:(h + 1) * 128], PT[:, 1, 0:Tn], start=False, stop=True)
            P.act(Rm[:, 0:Tn], bD[:, 0:Tn], AF.Ln)
            P.act(Rm[:, 0:Tn], Rm[:, 0:Tn], AF.Exp, scale=-1.0)
            P.tt(self.om[:, h, 0:Tn], bO[:, 0:Tn], Rm[:, 0:Tn], ALU.mult)

    def make_a(self, l, ctx):
        P, d = self.P, self.d
        TA, NA = ctx["TA"], ctx["NA"]
        samp = ctx["kind"] == "s"
        cAq = self.wchunk(d["w_in"], l, C_AQ, 512)
        cAkv = self.wchunk(d["w_in"], l, C_AK, 256)
        crow = self.crow
        sel2 = crow[0:2, 0:128]
        mpl = crow[0:1, 128:256]
        mcl = crow[0:1, 256:384]
        mpr = crow[0:1, 384:896]
        mcr = crow[0:1, 896:1408]
        if samp:
            stg = self.xstage[:, 1, 0:128]
            P.dma("sync", stg, d["cwk"][l])
            bk = self.pbank()
            P.tr(bk[:, 0:128], stg, self.identf[:, :])
            P.copy(self.KTwin[:, l, 0, :], bk[:, 0:128])
            P.dma("gpsimd", self.Vwin[:, l, 0, 0, 0:64], d["cwv"][l][:, 0:64])
            P.dma("gpsimd", self.Vwin[:, l, 0, 1, 64:128], d["cwv"][l][:, 64:128])
            P.dma("sync", d["wks"][l, 0:96, :], d["cwk"][l, 32:128, :])
            P.dma("sync", d["wvs"][l, 0:96, :], d["cwv"][l, 32:128, :])
        tiles = {}
        a_base = self.acur

        def stage1(i):
                self.acur = a_base
                if samp:
                    gi, slot, prev, has_prev, masked, rt = 1, 1, 0, True, False, 16
                else:
                    gi = ctx["st"] * 4 + i
                    slot, prev, has_prev, masked, rt = gi % 3, (gi - 1) % 3, gi > 0, True, gi
                last = samp or gi == 15
                qtok = self.abf(512).rearrange("p (c g e) -> p c g e", c=4, g=2)
                ktokb = self.abf(128)
                QT = self.abf(512).rearrange("p (c t) -> p c t", c=4)
                PT = [[self.abf(512) for _ in range(2)] for _ in range(2)]
                ktokf = self.af32(128)
                vtokf = self.af32(128)
                tq = [self.af32(64).rearrange("p (g c e) -> p g c e", g=2, c=4) for _ in range(4)]
                tk = [self.af32(16).rearrange("p (h e) -> p h e", h=2) for _ in range(4)]
                Ra = self.af32(512)
                bQ = self.pbank()
                bKV = self.pbank()
                xt = slice(i * TA, (i + 1) * TA)
                for k in range(8):
                    P.mm(bQ[0:TA, :], self.xb[:, k, xt], cAq[:, k, :], start=(k == 0), stop=(k == 7))
                for k in range(8):
                    P.mm(bKV[0:TA, 0:256], self.xb[:, k, xt], cAkv[:, k, :], start=(k == 0), stop=(k == 7))
                cos = self.rope[0:TA, rt, 0:8]
                sin = self.rope[0:TA, rt, 8:16]
                qv = bQ[0:TA, :].rearrange("p (g c e) -> p g c e", g=2, c=4)
                qo = qtok[0:TA].rearrange("p c g e -> p g c e")
                cq = cos.unsqueeze(1).unsqueeze(1).to_broadcast([TA, 2, 4, 8])
                sq_ = sin.unsqueeze(1).unsqueeze(1).to_broadcast([TA, 2, 4, 8])
                t0, t1, t2, t3 = [t[0:TA] for t in tq]
                P.tt(t0, qv[:, :, :, 0:8], cq, ALU.mult)
                P.tt(t1, qv[:, :, :, 8:16], sq_, ALU.mult)
                P.tt(qo[:, :, :, 0:8], t0, t1, ALU.subtract)
                P.tt(t2, qv[:, :, :, 8:16], cq, ALU.mult)
                P.tt(t3, qv[:, :, :, 0:8], sq_, ALU.mult)
                P.tt(qo[:, :, :, 8:16], t2, t3, ALU.add)
                P.copy(qo[:, :, :, 16:64], qv[:, :, :, 16:64], eng="scalar")
                kv = bKV[0:TA, 0:128].rearrange("p (h e) -> p h e", h=2)
                ko = ktokf[0:TA].rearrange("p (h e) -> p h e", h=2)
                ck = cos.unsqueeze(1).to_broadcast([TA, 2, 8])
                sk = sin.unsqueeze(1).to_broadcast([TA, 2, 8])
                u0, u1, u2, u3 = [t[0:TA] for t in tk]
                P.tt(u0, kv[:, :, 0:8], ck, ALU.mult)
                P.tt(u1, kv[:, :, 8:16], sk, ALU.mult)
                P.tt(ko[:, :, 0:8], u0, u1, ALU.subtract)
                P.tt(u2, kv[:, :, 8:16], ck, ALU.mult)
                P.tt(u3, kv[:, :, 0:8], sk, ALU.mult)
                P.tt(ko[:, :, 8:16], u2, u3, ALU.add)
                P.copy(ko[:, :, 16:64], kv[:, :, 16:64], eng="scalar")
                P.copy(ktokb[0:TA], ktokf[0:TA], eng="scalar")
                vdiag = self.Vwin[0:TA, l, slot].rearrange("p j (g e) -> p (j g) e", g=2)[:, 0::3, :]
                P.copy(vdiag, bKV[0:TA, 128:256].rearrange("p (g e) -> p g e", g=2), eng="vector")
                if last:
                    P.copy(vtokf[0:TA], bKV[0:TA, 128:256], eng="scalar")
                    if samp:
                        P.dma("sync", d["wks"][l, 96:128, :], ktokf[0:TA])
                        P.dma("sync", d["wvs"][l, 96:128, :], vtokf[0:TA])
                    else:
                        P.dma("sync", d["wkp"][l, ctx["b"]], ktokf[0:TA])
                        P.dma("sync", d["wvp"][l, ctx["b"]], vtokf[0:TA])
                tiles[i] = dict(QT=QT, PT=PT, Ra=Ra, slot=slot, prev=prev, has_prev=has_prev, masked=masked, xt=xt,
                                qtok=qtok, ktokb=ktokb)

        def stage1b(i):
                tl = tiles[i]
                qtok, ktokb, QT, slot = tl["qtok"], tl["ktokb"], tl["QT"], tl["slot"]
                bT = self.pbank().bitcast(BF16)
                for c in range(4):
                    P.tr(bT[:, c * 128:c * 128 + TA], qtok[0:TA, c].rearrange("p g e -> p (g e)"), self.identb[0:TA, 0:TA])
                P.tr(bT[:, 512:512 + TA], ktokb[0:TA], self.identb[0:TA, 0:TA])
                P.copy(QT[:, :, 0:TA], bT[:, 0:512].rearrange("p (c t) -> p c t", c=4)[:, :, 0:TA], eng="vector")
                P.copy(self.KTwin[:, l, slot, 0:TA], bT[:, 512:512 + TA], eng="scalar")

        def stage2(i):
            tl = tiles[i]
            QT, PT, Ra, slot, prev = tl["QT"], tl["PT"], tl["Ra"], tl["slot"], tl["prev"]
            has_prev, masked, xt = tl["has_prev"], tl["masked"], tl["xt"]
            NQ = 4 * TA
            for g in range(2):
                rows = slice(g * 64, (g + 1) * 64)
                qr = QT[rows, :, 0:TA]
                if has_prev:
                    bS = self.pbank()
                    P.mm(bS[:, 0:NQ], self.KTwin[rows, l, prev, :], qr, start=True, stop=(not masked))
                    if masked:
                        P.mm(bS[:, 0:NQ], mpl, mpr, start=False, stop=True)
                    P.act(PT[g][0][:, 0:NQ], bS[:, 0:NQ], AF.Exp, scale=A_SCALE)
                bS = self.pbank()
                P.mm(bS[0:TA, 0:NQ], self.KTwin[rows, l, slot, 0:TA], qr, start=True, stop=(not masked))
                if masked:
                    P.mm(bS[0:TA, 0:NQ], mcl, mcr, start=False, stop=True)
                P.act(PT[g][1][0:TA, 0:NQ], bS[0:TA, 0:NQ], AF.Exp, scale=A_SCALE)

        def stage2b(i):
            tl = tiles.pop(i)
            QT, PT, Ra, slot, prev = tl["QT"], tl["PT"], tl["Ra"], tl["slot"], tl["prev"]
            has_prev, masked, xt = tl["has_prev"], tl["masked"], tl["xt"]
            NQ = 4 * TA
            bD = self.pbank()
            first = True
            if has_prev:
                P.mm(bD[:, 0:NQ], self.ones_lo, PT[0][0][:, 0:NQ], start=True, stop=False)
                P.mm(bD[:, 0:NQ], self.ones_hi, PT[1][0][:, 0:NQ], start=False, stop=False)
                first = False
            P.mm(bD[:, 0:NQ], self.ones_lo[0:TA, :], PT[0][1][0:TA, 0:NQ], start=first, stop=False)
            P.mm(bD[:, 0:NQ], self.ones_hi[0:TA, :], PT[1][1][0:TA, 0:NQ], start=False, stop=False)
            P.mm(bD[:, 0:NQ], sel2, self.sinkrow[0:2, l, :, 0:TA], start=False, stop=True)
            bO = self.pbank()
            for c in range(4):
                cs = slice(c * TA, (c + 1) * TA)
                first = True
                if has_prev:
                    P.mm(bO[:, cs], self.Vwin[:, l, prev, 0, :], PT[0][0][:, cs], start=True, stop=False)
                    P.mm(bO[:, cs], self.Vwin[:, l, prev, 1, :], PT[1][0][:, cs], start=False, stop=False)
                    first = False
                P.mm(bO[:, cs], self.Vwin[0:TA, l, slot, 0, :], PT[0][1][0:TA, cs], start=first, stop=False)
                P.mm(bO[:, cs], self.Vwin[0:TA, l, slot, 1, :], PT[1][1][0:TA, cs], start=False, stop=True)
            P.act(Ra[:, 0:NQ], bD[:, 0:NQ], AF.Ln)
            P.act(Ra[:, 0:NQ], Ra[:, 0:NQ], AF.Exp, scale=-1.0)
            P.tt(self.oa[:, :, xt], bO[:, 0:NQ].rearrange("p (c t) -> p c t", c=4),
                 Ra[:, 0:NQ].rearrange("p (c t) -> p c t", c=4), ALU.mult)


        return stage1, stage1b, stage2, stage2b

    def phase_ab(self, l, ctx):
        P, d = self.P, self.d
        TA, NA, Tn = ctx["TA"], ctx["NA"], ctx["T"]
        samp = ctx["kind"] == "s"
        cBq = self.wchunk(d["w_in"], l, C_BQ, 512)
        cBk = self.wchunk(d["w_in"], l, C_BK, 512)
        cBv0 = self.wchunk(d["w_in"], l, C_BV, 512)
        cBv1 = self.wchunk(d["w_in"], l, C_BV + 512, 512)
        cBgk = self.wchunk(d["w_in"], l, C_BGK, 16)
        stl = self.state[:, l]
        if samp:
            P.dma("sync", stl, d["sg"][l].rearrange("h k v -> k h v"))
        elif ctx["st"] == 0:
            P.memset(stl, 0.0)
        P.copy(self.state_bf[:], stl, eng="scalar")
        self.arena_reset()
        sp = self.af32(512)
        Eq = self.af32(512).rearrange("p (h t) -> p h t", h=4)
        Ek = self.af32(512).rearrange("p (h t) -> p h t", h=4)
        Er = self.af32(512)
        e_t = Er
        srt = self.af32(512).rearrange("p (h t) -> p h t", h=4)
        rp = srt
        attT = self.abf(512).rearrange("p (h t) -> p h t", h=4)
        sq = self.abf(1024).rearrange("p (c t) -> p c t", c=8)
        cross = []
        for _ in range(1):
            cross.append(dict(qdT=self.abf(512).rearrange("p (h t) -> p h t", h=4),
                              kiT=self.abf(512).rearrange("p (h t) -> p h t", h=4),
                              kend=self.abf(512), vtok=self.abf(1024), elast=self.af32(4)))
        obraw = self.xstage[:, 0, :].rearrange("p (c t) -> p c t", c=8)
        qraw, kraw = self.mg[:, 0:4, :], self.mg[:, 4:8, :]
        a_s1a, a_s1b, a_s2a, a_s2b = self.make_a(l, ctx)
        b0 = self.pbank()
        for k in range(8):
            P.mm(b0[0:16, 0:Tn], cBgk[:, k, 0:16], self.xb[:, k, 0:Tn], start=(k == 0), stop=(k == 7))
        P.copy(self.bgkT[0:16, 0:Tn], b0[0:16, 0:Tn], eng="vector")
        for j, (cw, dst) in enumerate(((cBq, qraw), (cBk, kraw))):
            bks = [self.pbank() for _ in range(4)]
            for k in range(8):
                for h in range(4):
                    P.mm(bks[h][:, 0:Tn], cw[:, k, h * 128:(h + 1) * 128], self.xb[:, k, 0:Tn], start=(k == 0), stop=(k == 7))
            for h in range(4):
                P.copy(dst[:, h, 0:Tn], bks[h][:, 0:Tn], eng=("vector" if (h + j) % 2 == 0 else "scalar"))

        st1 = {}

        def stage1z(i):
            xt = slice(i * TA, (i + 1) * TA)
            bZ = self.pbank()
            P.mm(bZ[0:TA, :], self.bgkT[0:17, xt], self.wgk[0:17, l, :])
            P.act(e_t[0:TA], bZ[0:TA, :], AF.Exp, scale=-1.0)
            P.act(sp[0:TA], e_t[0:TA], AF.Ln, bias=1.0)

        def stage1(i):
            c = cross[0]
            qdT, kiT, kend, vtok, elast = c["qdT"], c["kiT"], c["kend"], c["vtok"], c["elast"]
            xt = slice(i * TA, (i + 1) * TA)
            for hv, cw in enumerate((cBv0, cBv1)):
                bV = self.pbank()
                for k in range(8):
                    P.mm(bV[0:TA, :], self.xb[:, k, xt], cw[:, k, :], start=(k == 0), stop=(k == 7))
                P.copy(vtok[0:TA, hv * 512:(hv + 1) * 512], bV[0:TA, :], eng=("vector" if hv == 0 else "scalar"))

        def stage1b(i):
            c = cross[0]
            qdT, kiT, kend, vtok, elast = c["qdT"], c["kiT"], c["kend"], c["vtok"], c["elast"]
            xt = slice(i * TA, (i + 1) * TA)
            bC = self.pbank()
            for h in range(4):
                P.mm(bC[:, h * 128:h * 128 + TA], sp[0:TA, h * 128:(h + 1) * 128], self.tri[0:TA, 0, 0:TA])
            bR = self.pbank()
            P.mm(bR[0:TA, :], self.tri[0:TA, 1, 0:TA], sp[0:TA, :])
            bKt = self.pbank()
            for k in range(8):
                P.mm(bKt[0:TA, :], self.xb[:, k, xt], cBk[:, k, :], start=(k == 0), stop=(k == 7))
            bCv = bC.rearrange("p (h t) -> p h t", h=4)[:, :, 0:TA]
            P.act(Eq[:, :, 0:TA], bCv, AF.Exp)
            P.act(Ek[:, :, 0:TA], bCv, AF.Exp, scale=-1.0)
            P.act(Er[0:TA], bR[0:TA, :], AF.Exp)
            P.copy(elast[:, 0:4], Eq[:, :, TA - 1], eng="scalar")
            P.tt(kend[0:TA], bKt[0:TA, :], Er[0:TA], ALU.mult)
            P.stt(qdT[:, :, 0:TA], qraw[:, :, xt], BQ_SCALE, Eq[:, :, 0:TA], ALU.mult, ALU.mult)
            P.tt(kiT[:, :, 0:TA], kraw[:, :, xt], Ek[:, :, 0:TA], ALU.mult)

        def stage2(i):
            c = cross[0]
            qdT, kiT, kend, vtok, elast = c["qdT"], c["kiT"], c["kend"], c["vtok"], c["elast"]
            xt = slice(i * TA, (i + 1) * TA)
            bA = self.pbank()
            for h in range(4):
                P.mm(bA[0:TA, h * 128:h * 128 + TA], kiT[:, h, 0:TA], qdT[:, h, 0:TA])
            P.tt(attT[0:TA, :, 0:TA], bA[0:TA].rearrange("p (h t) -> p h t", h=4)[:, :, 0:TA],
                 self.cmask[0:TA, 0:TA].unsqueeze(1).to_broadcast([TA, 4, TA]), ALU.mult)

        def stage2b(i):
            c = cross[0]
            qdT, kiT, kend, vtok, elast = c["qdT"], c["kiT"], c["kend"], c["vtok"], c["elast"]
            xt = slice(i * TA, (i + 1) * TA)
            bOs = [self.pbank(), self.pbank()]
            for h in range(4):
                for hf in range(2):
                    reg = bOs[h // 2][:, ((h % 2) * 2 + hf) * 128:((h % 2) * 2 + hf) * 128 + TA]
                    P.mm(reg, vtok[0:TA, h * 256 + hf * 128:h * 256 + (hf + 1) * 128], attT[0:TA, h, 0:TA],
                         start=True, stop=False)
                    P.mm(reg, self.state_bf[:, h, hf * 128:(hf + 1) * 128], qdT[:, h, 0:TA], start=False, stop=True)
            for j in range(2):
                ov = bOs[j].rearrange("p (c t) -> p c t", c=4)[:, :, 0:TA]
                P.act(obraw[:, j * 4:(j + 1) * 4, 0:TA], ov, AF.Copy)
                P.act(sq[:, j * 4:(j + 1) * 4, 0:TA], ov, AF.Square)
            bDs = [self.pbank(), self.pbank()]
            for h in range(4):
                reg = bDs[h // 2][:, (h % 2) * 256:(h % 2 + 1) * 256]
                P.mm(reg, kend[0:TA, h * 128:(h + 1) * 128], vtok[0:TA, h * 256:(h + 1) * 256])
            for h in range(4):
                reg = bDs[h // 2][:, (h % 2) * 256:(h % 2 + 1) * 256]
                P.stt(stl[:, h, :], stl[:, h, :], elast[:, h:h + 1], reg, ALU.mult, ALU.add)
            if i + 1 < NA:
                P.copy(self.state_bf[:], stl, eng="scalar")

        def stage2c(i):
            xt = slice(i * TA, (i + 1) * TA)
            bS = self.pbank()
            for h in range(4):
                P.mm(bS[:, h * 128:h * 128 + TA], self.ones, sq[:, 2 * h, 0:TA], start=True, stop=False)
                P.mm(bS[:, h * 128:h * 128 + TA], self.ones, sq[:, 2 * h + 1, 0:TA], start=False, stop=True)
            P.act(srt[:, :, 0:TA], bS.rearrange("p (h t) -> p h t", h=4)[:, :, 0:TA], AF.Ln,
                  bias=RMS_EPS, scale=1.0 / 256.0)
            P.act(rp[:, :, 0:TA], srt[:, :, 0:TA], AF.Exp, scale=-0.5)
            for h in range(4):
                P.tt(self.ob[:, 2 * h:2 * h + 2, xt], obraw[:, 2 * h:2 * h + 2, 0:TA],
                     rp[:, h, 0:TA].unsqueeze(1).to_broadcast([128, 2, TA]), ALU.mult)

        a_s1a(0)
        for i in range(NA):
            stage1z(i)
            if i > 0:
                stage2b(i - 1)
            stage1(i)
            a_s1b(i)
            if i > 0:
                stage2c(i - 1)
            stage1b(i)
            a_s2a(i)
            stage2(i)
            if i + 1 < NA:
                a_s1a(i + 1)
            a_s2b(i)
        stage2b(NA - 1)
        stage2c(NA - 1)
        if samp:
            P.dma("sync", d["gss"][l].rearrange("h k v -> k h v"), stl)
        elif ctx["st"] == 3:
            P.dma("sync", d["gsp"][l, ctx["b"]].rearrange("h k v -> k h v"), stl)

    def phase_b2(self, l, ctx):
        P, d = self.P, self.d
        Tn = ctx["T"]
        cBg = [self.wchunk(d["w_in"], l, C_BG, 512), self.wchunk(d["w_in"], l, C_BG + 512, 512)]
        self.arena_reset()
        gss = [self.af32(512) for _ in range(2)]
        for c in range(8):
            bk = self.pbank()
            for k in range(8):
                P.mm(bk[:, 0:Tn], cBg[c // 4][:, k, (c % 4) * 128:(c % 4 + 1) * 128], self.xb[:, k, 0:Tn],
                     start=(k == 0), stop=(k == 7))
            gs = gss[c % 2]
            P.act(gs[:, 0:Tn], bk[:, 0:Tn], AF.Silu)
            P.stt(self.ob[:, c, 0:Tn], gs[:, 0:Tn], self.gng[:, l, c % 2:c % 2 + 1], self.ob[:, c, 0:Tn],
                  ALU.mult, ALU.mult)

    def phase_merge(self, l, ctx):
        P, d = self.P, self.d
        Tn = ctx["T"]
        self.arena_reset()
        sgs = [[self.af32(512) for _ in range(3)] for _ in range(2)]
        if self.dbg and ctx.get("b", 0) == 0 and ctx["st"] == 0:
            tg = "%s%d" % (ctx["kind"], l)
            self.dbg_dump_bf("dbg_oa_" + tg, self.oa, 4, Tn)
            self.dbg_dump_bf("dbg_ob_" + tg, self.ob, 8, Tn)
            self.dbg_dump_bf("dbg_om_" + tg, self.om, 4, Tn)
        for f in range(8):
            fs = slice(f * 128, (f + 1) * 128)
            pieces = []
            for g in range(2):
                src = d["w_proj_a"][l, g * 256:(g + 1) * 256, fs].rearrange("(c e) n -> e c n", e=64)
                pieces.append((0, 4, 128, src, g * 64, (g + 1) * 64))
            pieces.append((512, 8, 128, d["w_proj_b"][l, :, fs].rearrange("(k p) n -> p k n", p=128), 0, 128))
            pieces.append((1536, 4, 128, d["w_proj_m"][l, :, fs].rearrange("(k p) n -> p k n", p=128), 0, 128))
            cP = self.chunk(("P", l, f), pieces)
            pa = cP[:, 0:512].rearrange("p (k n) -> p k n", n=128)
            pb = cP[:, 512:1536].rearrange("p (k n) -> p k n", n=128)
            pm = cP[:, 1536:2048].rearrange("p (k n) -> p k n", n=128)
            pieces = []
            for j in range(3):
                c0 = C_G + j * 1024 + f * 128
                pieces.append((j * 1024, 8, 128, d["w_in"][l, :, c0:c0 + 128].rearrange("(k p) n -> p k n", p=128), 0, 128))
            cG = self.chunk(("G", l, f), pieces)
            gw = [cG[:, j * 1024:(j + 1) * 1024].rearrange("p (k n) -> p k n", n=128) for j in range(3)]
            sg = sgs[f % 2]
            bG = []
            for j in range(3):
                bk = self.pbank()
                for k in range(8):
                    P.mm(bk[:, 0:Tn], gw[j][:, k, :], self.xb[:, k, 0:Tn], start=(k == 0), stop=(k == 7))
                P.act(sg[j][:, 0:Tn], bk[:, 0:Tn], AF.Sigmoid)
            bPA = self.pbank()
            for c in range(4):
                P.mm(bPA[:, 0:Tn], pa[:, c, :], self.oa[:, c, 0:Tn], start=(c == 0), stop=(c == 3))
            P.tt(sg[0][:, 0:Tn], bPA[:, 0:Tn], sg[0][:, 0:Tn], ALU.mult)
            bPB = self.pbank()
            for c in range(8):
                P.mm(bPB[:, 0:Tn], pb[:, c, :], self.ob[:, c, 0:Tn], start=(c == 0), stop=(c == 7))
            P.tt(sg[1][:, 0:Tn], bPB[:, 0:Tn], sg[1][:, 0:Tn], ALU.mult)
            bPM = self.pbank()
            for c in range(4):
                P.mm(bPM[:, 0:Tn], pm[:, c, :], self.om[:, c, 0:Tn], start=(c == 0), stop=(c == 3))
            P.tt(sg[2][:, 0:Tn], bPM[:, 0:Tn], sg[2][:, 0:Tn], ALU.mult)
            P.tt(sg[0][:, 0:Tn], sg[0][:, 0:Tn], sg[1][:, 0:Tn], ALU.add)
            P.tt(self.mg[:, f, 0:Tn], sg[0][:, 0:Tn], sg[2][:, 0:Tn], ALU.add)
        if self.dbg and ctx.get("b", 0) == 0 and ctx["st"] == 0:
            self.dbg_dump_bf("dbg_mg_%s%d" % (ctx["kind"], l), self.mg, 8, Tn)

    def ln_prep(self, ctx):
        self.ln_ysq = [self.abf(512) for _ in range(2)]
        self.ln_bM = self.ps[:, 6, :]
        self.ln_bQ = self.ps[:, 7, :]

    def ln_act(self, f, ctx):
        P, Tn = self.P, ctx["T"]
        P.copy(self.xb[:, f, 0:Tn], self.xres[:, f, 0:Tn], eng="scalar")
        P.act(self.ln_ysq[f % 2][:, 0:Tn], self.xres[:, f, 0:Tn], AF.Square)

    def ln_mm(self, f, ctx):
        P, Tn = self.P, ctx["T"]
        P.mm(self.ln_bM[:, 0:Tn], self.onesm, self.xb[:, f, 0:Tn], start=(f == 0), stop=(f == 7))
        P.mm(self.ln_bQ[:, 0:Tn], self.onesm, self.ln_ysq[f % 2][:, 0:Tn], start=(f == 0), stop=(f == 7))

    def ln_finish(self, l, which, ctx):
        P = self.P
        Tn = ctx["T"]
        gi, bi = (0, 1) if which == 1 else (2, 3)
        bM, bQ = self.ln_bM, self.ln_bQ
        self.acur = 2048
        msb = self.af32(512)
        var = self.af32(512)
        rstd = self.af32(512)
        tmp = [self.af32(512) for _ in range(3)]
        P.copy(msb[:, 0:Tn], bM[:, 0:Tn], eng="scalar")
        P.act(var[:, 0:Tn], bM[:, 0:Tn], AF.Square)
        P.tt(var[:, 0:Tn], bQ[:, 0:Tn], var[:, 0:Tn], ALU.subtract)
        P.act(rstd[:, 0:Tn], var[:, 0:Tn], AF.Ln, bias=LN_EPS)
        P.act(rstd[:, 0:Tn], rstd[:, 0:Tn], AF.Exp, scale=-0.5)
        tmp = tmp + [self.af32(512) for _ in range(2)]

        def res_affine(f):
            P.act(self.xres[:, f, 0:Tn], tmp[f % 5][:, 0:Tn], AF.Identity, bias=self.lnp[:, l, bi, f:f + 1],
                  scale=self.lnp[:, l, gi, f:f + 1])

        for f in range(8):
            t = tmp[f % 5]
            P.tt(t[:, 0:Tn], self.xres[:, f, 0:Tn], msb[:, 0:Tn], ALU.subtract)
            P.tt(t[:, 0:Tn], t[:, 0:Tn], rstd[:, 0:Tn], ALU.mult)
            P.act(self.xb[:, f, 0:Tn], t[:, 0:Tn], AF.Identity, bias=self.lnp[:, l, bi, f:f + 1],
                  scale=self.lnp[:, l, gi, f:f + 1])
            if f >= 3:
                res_affine(f - 3)
        for f in range(5, 8):
            res_affine(f)

    def phase_wout(self, l, ctx):
        P, d = self.P, self.d
        Tn = ctx["T"]
        cWo = [self.wchunk(d["w_out"], l, 0, 1024, r0=0, kc=4), self.wchunk(d["w_out"], l, 0, 1024, r0=512, kc=4)]
        self.arena_reset()
        self.ln_prep(ctx)
        for f in range(8):
            bk = self.pbank()
            for k in range(8):
                P.mm(bk[:, 0:Tn], cWo[k // 4][:, k % 4, f * 128:(f + 1) * 128], self.mg[:, k, 0:Tn],
                     start=(k == 0), stop=(k == 7))
            P.stt(self.xres[:, f, 0:Tn], self.xres[:, f, 0:Tn], ALPHA, bk[:, 0:Tn], ALU.mult, ALU.add)
            self.ln_act(f, ctx)
            if f > 0:
                self.ln_mm(f - 1, ctx)
        self.ln_mm(7, ctx)
        self.ln_finish(l, 1, ctx)
        if self.dbg and ctx.get("b", 0) == 0 and ctx["st"] == 0:
            self.dbg_dump("dbg_x1_%s%d" % (ctx["kind"], l), self.xres[:, :, 0:Tn], (128, 8, Tn))

    def phase_mlp(self, l, ctx):
        P, d = self.P, self.d
        Tn = ctx["T"]
        h1 = self.ob
        for f in range(8):
            P.ts(self.xres[:, f, 0:Tn], self.xres[:, f, 0:Tn], ALPHA, ALU.mult, self.lnp[:, l, 4, f:f + 1], ALU.add)
        self.arena_reset()
        self.ln_prep(ctx)
        rt = [self.abf(512) for _ in range(2)]
        for g in range(4):
            cU = [self.wchunk(d["w_up"], l, g * 1024, 512), self.wchunk(d["w_up"], l, g * 1024 + 512, 512)]
            pre = {}
            if g == 0:
                for fc in range(4):
                    pre[fc] = self.pbank()
                for k in range(8):
                    for fc in range(4):
                        P.mm(pre[fc][:, 0:Tn], cU[0][:, k, fc * 128:(fc + 1) * 128], self.xb[:, k, 0:Tn],
                             start=(k == 0), stop=(k == 7))
            for fc in range(8):
                if fc in pre:
                    bk = pre[fc]
                else:
                    bk = self.pbank()
                    for k in range(8):
                        P.mm(bk[:, 0:Tn], cU[fc // 4][:, k, (fc % 4) * 128:(fc % 4 + 1) * 128], self.xb[:, k, 0:Tn],
                             start=(k == 0), stop=(k == 7))
                r = rt[fc % 2]
                P.act(r[:, 0:Tn], bk[:, 0:Tn], AF.Relu, bias=self.bup[:, l, g * 8 + fc:g * 8 + fc + 1])
                P.tt(h1[:, fc, 0:Tn], r[:, 0:Tn], r[:, 0:Tn], ALU.mult)
            cD = [self.wchunk(d["w_down"], l, 0, 1024, r0=g * 1024, kc=4),
                  self.wchunk(d["w_down"], l, 0, 1024, r0=g * 1024 + 512, kc=4)]
            for f in range(8):
                bk = self.pbank()
                for k in range(8):
                    P.mm(bk[:, 0:Tn], cD[k // 4][:, k % 4, f * 128:(f + 1) * 128], h1[:, k, 0:Tn],
                         start=(k == 0), stop=(k == 7))
                P.tt(self.xres[:, f, 0:Tn], self.xres[:, f, 0:Tn], bk[:, 0:Tn], ALU.add)
                if g == 3:
                    self.ln_act(f, ctx)
                    if f > 0:
                        self.ln_mm(f - 1, ctx)
        self.ln_mm(7, ctx)


def _small_params(inp):
    f = np.float32
    lnp = np.zeros((128, NL, 5, 8), f)
    for j, nm in enumerate(("ln1_g", "ln1_b", "ln2_g", "ln2_b", "b_down")):
        lnp[:, :, j, :] = np.asarray(inp[nm], f).reshape(NL, 8, 128).transpose(2, 0, 1)
    bup = np.ascontiguousarray(np.asarray(inp["b_up"], f).reshape(NL, 32, 128).transpose(2, 0, 1))
    gng = np.ascontiguousarray(np.asarray(inp["gla_norm_g"], f).reshape(NL, 2, 128).transpose(2, 0, 1))
    wgk = np.zeros((17, NL, 512), f)
    wgk[0:16] = np.asarray(inp["w_gk2"], f).transpose(1, 0, 2)
    wgk[16] = np.asarray(inp["b_gk"], f)
    sinks = np.ascontiguousarray(np.asarray(inp["attn_sinks"], f).reshape(NL, 2, 4).transpose(1, 0, 2))
    return dict(lnp=lnp, bup=bup, gng=gng, wgk2aug=wgk, sinks=sinks)


def make_in_maps(inp, n_cores=8, n_seq=4):
    f = np.float32
    shared = {k: np.ascontiguousarray(np.asarray(inp[k], f)) for k in
              ("w_in", "w_mem_kv", "w_proj_a", "w_proj_b", "w_proj_m", "w_out", "w_up", "w_down")}
    shared.update(_consts())
    shared.update(_small_params(inp))
    maps = []
    for c in range(n_cores):
        m = dict(shared)
        m["xp"] = np.ascontiguousarray(np.asarray(inp["x_prompt"][c * n_seq:(c + 1) * n_seq], f))
        m["memp"] = np.ascontiguousarray(np.asarray(inp["mem_prompt"][c * n_seq:(c + 1) * n_seq], f))
        m["xs"] = np.ascontiguousarray(np.asarray(inp["x_sample"][c], f))
        m["cwk"] = np.ascontiguousarray(np.asarray(inp["cache_win_k"][:, c], f).reshape(NL, 128, 128))
        m["cwv"] = np.ascontiguousarray(np.asarray(inp["cache_win_v"][:, c], f).reshape(NL, 128, 128))
        m["sg"] = np.ascontiguousarray(np.asarray(inp["state_gla"][:, c], f))
        m["cmk"] = np.ascontiguousarray(np.asarray(inp["cache_mem_k"][:, c], f).reshape(NL, 256, 512))
        m["cmv"] = np.ascontiguousarray(np.asarray(inp["cache_mem_v"][:, c], f).reshape(NL, 256, 512))
        maps.append(m)
    return maps


def kernel(**inp):
    bld = Builder(n_seq=4, n_layers=NL, do_sample=True, n_st=4)
    nc = bld.build()
    maps = make_in_maps(inp, 8, 4)
    res = run_bass_kernel_spmd(nc, maps, core_ids=list(range(8)))
    R = res.results
    f = np.float32
    y_p = np.concatenate([r["yp"] for r in R], axis=0).astype(f)
    y_s = np.stack([r["ys"] for r in R], axis=0).astype(f)
    wkp = np.concatenate([r["wkp"] for r in R], axis=1).reshape(NL, NB, 128, 2, 64).astype(f)
    wvp = np.concatenate([r["wvp"] for r in R], axis=1).reshape(NL, NB, 128, 2, 64).astype(f)
    gsp = np.concatenate([r["gsp"] for r in R], axis=1).astype(f)
    mkp = np.concatenate([r["mkp"] for r in R], axis=1).reshape(NL, NB, 256, 4, 128).astype(f)
    mvp = np.concatenate([r["mvp"] for r in R], axis=1).reshape(NL, NB, 256, 4, 128).astype(f)
    wks = np.stack([r["wks"] for r in R], axis=1).reshape(NL, 8, 128, 2, 64).astype(f)
    wvs = np.stack([r["wvs"] for r in R], axis=1).reshape(NL, 8, 128, 2, 64).astype(f)
    gss = np.stack([r["gss"] for r in R], axis=1).astype(f)
    return (y_p, y_s, wkp, wvp, gsp, mkp, mvp, wks, wvs, gss)
```

```python
import numpy as np
from contextlib import ExitStack
import concourse.bass as bass
import concourse.mybir as mybir
from concourse.bass_utils import run_bass_kernel_spmd

F32 = mybir.dt.float32
BF16 = mybir.dt.bfloat16
AF = mybir.ActivationFunctionType
ALU = mybir.AluOpType
ENGS = ("tensor", "vector", "scalar", "gpsimd", "sync")
NDMASEM = 20

D = 1024
NL = 4
SEQ = 2048
NB = 32
TS = 32
PAST = 1024
DIN = 7440
C_AQ, C_AK, C_AV, C_BQ, C_BK, C_BV, C_BG, C_BGK, C_MQ, C_G = 0, 512, 640, 768, 1280, 1792, 2816, 3840, 3856, 4368
ALPHA = float(8 ** 0.25)
A_SCALE = 64 ** -0.5
M_SCALE = 128 ** -0.5
BQ_SCALE = 128 ** -0.5
LN_EPS = 1e-5
RMS_EPS = 1e-6
T = 512
RING = 8
ARENA = 7680


def _is_ap(x):
    return hasattr(x, "tensor") and hasattr(x, "ap") and hasattr(x, "offset")


class Prog:
    def __init__(self, nc):
        self.nc = nc
        self.ops = []
        self.ev = {}
        self.dma_cnt = {e: 0 for e in ENGS}
        self.dma_hist = {e: [] for e in ENGS}
        self.bank_i = 0

    def region(self, ap):
        t = ap.tensor
        if type(t).__name__.startswith("DRam"):
            return None
        dims = ap.ap
        esz = mybir.dt.size(ap.dtype)
        row = dims[0][0]
        npart = dims[0][1]
        off = ap.offset
        if row > 0:
            p0 = off // row
            f0 = off % row
        else:
            p0 = 0
            f0 = off
        ext = 1
        for s, c in dims[1:]:
            ext += (c - 1) * abs(s)
        return (t.name, p0, p0 + npart, f0 * esz, (f0 + ext) * esz)

    def _access(self, idx, eng, is_dma, regs, is_write, deps):
        for r in regs:
            if r is None:
                continue
            name, plo, phi, blo, bhi = r
            lst = self.ev.get(name, [])
            keep = []
            for e in lst:
                ov = (e[0] < phi and plo < e[1] and e[2] < bhi and blo < e[3])
                if ov and (is_write or e[5]):
                    if e[4] != idx:
                        deps.add(e[4])
                if is_write and ov and e[0] >= plo and e[1] <= phi and e[2] >= blo and e[3] <= bhi and e[4] != idx:
                    continue
                keep.append(e)
            key = None if (is_write or is_dma) else (eng, plo, phi, blo, bhi)
            if key is not None:
                keep = [e for e in keep if e[6] != key]
            keep.append([plo, phi, blo, bhi, idx, is_write, key])
            self.ev[name] = keep

    def op(self, eng, fn, reads=(), writes=(), dma=False):
        idx = len(self.ops)
        deps = set()
        rr = [self.region(a) if _is_ap(a) else (a, 0, 128, 0, 1 << 40) for a in reads if a is not None]
        ww = [self.region(a) if _is_ap(a) else (a, 0, 128, 0, 1 << 40) for a in writes if a is not None]
        pr = []
        for lst in (rr, ww):
            for r in list(lst):
                if r is not None and r[0] == "ps":
                    lst.remove(r)
                    b0, b1 = r[3] // 2048, (r[4] - 1) // 2048
                    pr.append(("ps", 0, 128, b0 * 2048, (b1 + 1) * 2048))
        self._access(idx, eng, dma, rr, False, deps)
        self._access(idx, eng, dma, ww, True, deps)
        self._access(idx, eng, dma, pr, True, deps)
        o = dict(eng=eng, fn=fn, deps=deps, dma=dma, sig=False)
        if dma:
            n = self.dma_cnt[eng]
            self.dma_cnt[eng] = n + 1
            hist = self.dma_hist[eng]
            if n >= NDMASEM:
                deps.add(hist[n - NDMASEM])
            hist.append(idx)
            o["semval"] = ("d", (eng, n % NDMASEM), 16 * (n // NDMASEM + 1))
        self.ops.append(o)
        return idx

    def bank(self):
        b = self.bank_i
        self.bank_i = (b + 1) % 6
        return b

    def emit(self, stack):
        nc = self.nc
        ops = self.ops
        for o in ops:
            if o["eng"] == "tensor" and not o["dma"]:
                o["deps"] = {d for d in o["deps"] if not (ops[d]["eng"] == "tensor" and not ops[d]["dma"])}
        for o in ops:
            for d in o["deps"]:
                if not ops[d]["dma"]:
                    ops[d]["sig"] = True
        eng_cnt = {e: 0 for e in ENGS}
        for o in ops:
            if (not o["dma"]) and o["sig"]:
                eng_cnt[o["eng"]] += 1
                o["semval"] = ("e", o["eng"], eng_cnt[o["eng"]])
        esem = {e: stack.enter_context(nc.semaphore("es_" + e)) for e in ENGS}
        dsem = {}
        dfinal = {}
        for e in ENGS:
            for i in range(min(NDMASEM, self.dma_cnt[e])):
                dsem[(e, i)] = stack.enter_context(nc.semaphore("ds_%s_%d" % (e, i)))
        for o in ops:
            if o["dma"]:
                _, k, v = o["semval"]
                dfinal[k] = max(dfinal.get(k, 0), v)
        per_eng = {e: [] for e in ENGS}
        for o in ops:
            per_eng[o["eng"]].append(o)
        block = stack.enter_context(nc.Block())

        def make(ename):
            def body(eng):
                waited = {}
                for o in per_eng[ename]:
                    need = {}
                    for d in o["deps"]:
                        kind, key, val = ops[d]["semval"]
                        kk = (kind, key)
                        if waited.get(kk, 0) >= val:
                            continue
                        if need.get(kk, 0) < val:
                            need[kk] = val
                    for kk, val in need.items():
                        sem = esem[kk[1]] if kk[0] == "e" else dsem[kk[1]]
                        eng.wait_ge(sem, val)
                        waited[kk] = val
                    ins = o["fn"](eng)
                    if o["dma"]:
                        ins.then_inc(dsem[o["semval"][1]], 16)
                    elif o["sig"]:
                        ins.then_inc(esem[ename], 1)
                if ename == "sync":
                    for k, v in dfinal.items():
                        eng.wait_ge(dsem[k], v)
                    for e in ENGS:
                        if eng_cnt[e]:
                            eng.wait_ge(esem[e], eng_cnt[e])
            return body

        block.tensor(make("tensor"))
        block.vector(make("vector"))
        block.scalar(make("scalar"))
        block.gpsimd(make("gpsimd"))
        block.sync(make("sync"))

    def mm(self, out, lhsT, rhs, start=True, stop=True):
        self.op("tensor", lambda e: e.matmul(out, lhsT=lhsT, rhs=rhs, start=start, stop=stop),
                reads=[lhsT, rhs], writes=[out])

    def tr(self, out, in_, ident):
        self.op("tensor", lambda e: e.transpose(out, in_, ident), reads=[in_, ident], writes=[out])

    def act(self, out, in_, func, bias=None, scale=None):
        kw = {}
        rd = [in_]
        if bias is not None:
            kw["bias"] = bias
            if _is_ap(bias):
                rd.append(bias)
        if scale is not None:
            kw["scale"] = scale
            if _is_ap(scale):
                rd.append(scale)
        self.op("scalar", lambda e: e.activation(out=out, in_=in_, func=func, **kw), reads=rd, writes=[out])

    def tt(self, out, in0, in1, op, eng="vector"):
        self.op(eng, lambda e: e.tensor_tensor(out=out, in0=in0, in1=in1, op=op), reads=[in0, in1], writes=[out])

    def ts(self, out, in0, s1, op0, s2=None, op1=None, eng="vector"):
        rd = [in0] + [s for s in (s1, s2) if _is_ap(s)]
        if op1 is None:
            self.op(eng, lambda e: e.tensor_scalar(out=out, in0=in0, scalar1=s1, scalar2=None, op0=op0),
                    reads=rd, writes=[out])
        else:
            self.op(eng, lambda e: e.tensor_scalar(out=out, in0=in0, scalar1=s1, scalar2=s2, op0=op0, op1=op1),
                    reads=rd, writes=[out])

    def stt(self, out, in0, scalar, in1, op0, op1):
        rd = [in0, in1] + ([scalar] if _is_ap(scalar) else [])
        self.op("vector", lambda e: e.scalar_tensor_tensor(out=out, in0=in0, scalar=scalar, in1=in1, op0=op0, op1=op1),
                reads=rd, writes=[out])

    def copy(self, out, in_, eng="vector"):
        if eng == "scalar":
            self.act(out, in_, AF.Copy)
        else:
            self.op(eng, lambda e: e.tensor_copy(out=out, in_=in_), reads=[in_], writes=[out])

    def recip(self, out, in_):
        self.op("vector", lambda e: e.reciprocal(out=out, in_=in_), reads=[in_], writes=[out])

    def memset(self, out, val, eng="vector"):
        self.op(eng, lambda e: e.memset(out, val), writes=[out])

    def dma(self, q, out, in_, reads=(), writes=()):
        self.op(q, lambda e: e.dma_start(out=out, in_=in_), reads=[in_] + list(reads), writes=[out] + list(writes),
                dma=True)


def _consts():
    c = {}
    c["c_ident"] = np.eye(128, dtype=np.float32)
    s = np.arange(128)[:, None]
    t = np.arange(128)[None, :]
    tri = np.zeros((128, 2, 128), np.float32)
    tri[:, 0, :] = np.where(s <= t, -1.0 / 16.0, 0.0)
    tri[:, 1, :] = np.where(s > t, -1.0 / 16.0, 0.0)
    c["c_tri"] = tri
    c["c_mask"] = np.where(s <= t, 1.0, 0.0).astype(np.float32)
    cb = np.zeros((128, 5, 128), np.float32)
    cb[:, 0, :] = np.eye(128)
    cb[:, 1, :] = 1.0
    cb[:, 2, 0:64] = 1.0
    cb[:, 3, 64:128] = 1.0
    cb[:, 4, :] = 1.0 / 1024.0
    c["c_bf"] = cb
    row = np.zeros((2, 1408), np.float32)
    row[0, 0:64] = 1.0
    row[1, 64:128] = 1.0
    row[0, 128:192] = 1.0
    row[0, 256 + 64:256 + 128] = 1.0
    q = np.arange(512) % 128
    row[0, 384:896] = np.where(q >= 64, -30000.0, 0.0)
    row[0, 896:1408] = np.where(q < 64, -30000.0, 0.0)
    c["c_row"] = row
    half = 8
    inv = (np.float32(500000.0) ** (-np.arange(half, dtype=np.float32) / np.float32(half))).astype(np.float32)
    rope = np.zeros((128, 17, 16), np.float32)
    for tile in range(17):
        if tile < 16:
            pos = (tile * 128 + np.arange(128)).astype(np.float32)
        else:
            pos = (PAST + np.arange(128)).astype(np.float32)
        ang = (pos[:, None] * inv[None, :]).astype(np.float32)
        rope[:, tile, 0:8] = np.cos(ang)
        rope[:, tile, 8:16] = np.sin(ang)
    c["c_rope"] = rope
    return c


class Builder:
    def __init__(self, n_seq=4, n_layers=NL, do_sample=True, n_st=4, dbg=False):
        self.n_seq = n_seq
        self.n_layers = n_layers
        self.do_sample = do_sample
        self.n_st = n_st
        self.dbg = dbg
        self.dbg_outs = {}

    def declare(self, nc):
        di = {}

        def I(name, shape):
            di[name] = nc.dram_tensor(name, list(shape), F32, kind="ExternalInput").ap()

        def O(name, shape):
            di[name] = nc.dram_tensor(name, list(shape), F32, kind="ExternalOutput").ap()

        ns = self.n_seq
        I("xp", (ns, SEQ, D)); I("xs", (TS, D)); I("cwk", (NL, 128, 128)); I("cwv", (NL, 128, 128))
        I("sg", (NL, 4, 128, 256)); I("cmk", (NL, 256, 512)); I("cmv", (NL, 256, 512)); I("memp", (ns, 256, D))
        I("w_in", (NL, D, DIN)); I("w_mem_kv", (NL, D, D)); I("w_proj_a", (NL, 512, D)); I("w_proj_b", (NL, D, D))
        I("w_proj_m", (NL, 512, D)); I("w_out", (NL, D, D)); I("w_up", (NL, D, 4096)); I("w_down", (NL, 4096, D))
        I("c_ident", (128, 128)); I("c_tri", (128, 2, 128)); I("c_mask", (128, 128)); I("c_bf", (128, 5, 128))
        I("c_row", (2, 1408)); I("c_rope", (128, 17, 16))
        I("lnp", (128, NL, 5, 8)); I("bup", (128, NL, 32)); I("gng", (128, NL, 2)); I("wgk2aug", (17, NL, 512))
        I("sinks", (2, NL, 4))
        O("yp", (ns, SEQ, D)); O("ys", (TS, D)); O("wkp", (NL, ns, 128, 128)); O("wvp", (NL, ns, 128, 128))
        O("gsp", (NL, ns, 4, 128, 256)); O("mkp", (NL, ns, 256, 512)); O("mvp", (NL, ns, 256, 512))
        O("wks", (NL, 128, 128)); O("wvs", (NL, 128, 128)); O("gss", (NL, 4, 128, 256))
        self.d = di
        self.wbf = nc.dram_tensor("wbf_scratch", [46 * NL, 128, 4096], BF16, kind="Internal").ap()
        self.wcache = {}

    def dbg_dump(self, name, ap, shape):
        if not self.dbg:
            return
        nc, P = self.nc, self.P
        o = nc.dram_tensor(name, list(shape), F32, kind="ExternalOutput").ap()
        self.dbg_outs[name] = tuple(shape)
        if ap.dtype == F32:
            P.dma("sync", o, ap)
        else:
            n = 1
            for s in shape[1:]:
                n *= s
            stg = self.dbgstage[0:shape[0], 0:n]
            if len(shape) == 3:
                stg = stg.rearrange("p (a b) -> p a b", b=shape[2])
            P.copy(stg, ap)
            P.dma("sync", o, stg)

    def dbg_dump_bf(self, name, t, C, Tn):
        if not self.dbg:
            return
        nc, P = self.nc, self.P
        o = nc.dram_tensor(name, [128, C, Tn], F32, kind="ExternalOutput").ap()
        for c in range(C):
            stg = self.dbgstage[:, (c % 2) * 512:(c % 2) * 512 + Tn]
            P.copy(stg, t[:, c, 0:Tn])
            P.dma("sync", o[:, c, :], stg)

    def build(self):
        nc = bass.Bass("TRN2", target_bir_lowering=False)
        self.nc = nc
        self.declare(nc)
        d = self.d
        with ExitStack() as st:
            def sb(name, shape, dt):
                return st.enter_context(nc.sbuf_tensor("s_" + name, list(shape), dt))

            P = Prog(nc)
            self.P = P
            self.ps = st.enter_context(nc.psum_tensor("ps", [128, 8, 512], F32))
            self.xres = sb("xres", (128, 8, T), F32)
            self.xb = sb("xb", (128, 8, T), BF16)
            self.oa = sb("oa", (128, 4, T), BF16)
            self.ob = sb("ob", (128, 8, T), BF16)
            self.om = sb("om", (128, 4, T), BF16)
            self.mg = sb("mg", (128, 8, T), BF16)
            self.ring_n = RING - (1 if self.dbg else 0)
            self.ring = sb("ring", (128, self.ring_n, 4096), BF16)
            self.ring_i = 0
            self.memT = sb("memT", (128, 8, 256), BF16)
            self.memkT = sb("memkT", (128, 4, 256), BF16)
            self.memv = sb("memv", (128, 2, 512), BF16)
            self.KTwin = sb("KTwin", (128, NL, 3, 128), BF16)
            self.Vwin = sb("Vwin", (128, NL, 3, 2, 128), BF16)
            self.state = sb("state", (128, NL, 4, 256), F32)
            self.state_bf = sb("state_bf", (128, 4, 256), BF16)
            self.xstage = sb("xstage", (128, 2, 1024), F32)
            self.ostage = sb("ostage", (128, 2, 512), F32)
            self.arena = sb("arena", (128, ARENA), F32)
            self.arena_bf = self.arena[:].bitcast(BF16)
            if self.dbg:
                self.dbgstage = sb("dbgstage", (128, 1024), F32)
            self.identf = sb("identf", (128, 128), F32)
            self.tri = sb("tri", (128, 2, 128), F32)
            self.cmask = sb("cmask", (128, 128), F32)
            self.cbf = sb("cbf", (128, 5, 128), BF16)
            self.crow = sb("crow", (2, 1408), BF16)
            self.rope = sb("rope", (128, 17, 16), F32)
            self.lnp = sb("lnp", (128, NL, 5, 8), F32)
            self.bup = sb("bup", (128, NL, 32), F32)
            self.gng = sb("gng", (128, NL, 2), F32)
            self.wgk = sb("wgk", (17, NL, 512), BF16)
            self.sinkf = sb("sinkf", (2, NL, 4), F32)
            self.sinke = sb("sinke", (2, NL, 4), F32)
            self.sinkrow = sb("sinkrow", (2, NL, 4, 128), BF16)
            self.bgkT = sb("bgkT", (17, 512), BF16)

            P.dma("sync", self.identf[:], d["c_ident"])
            P.dma("sync", self.tri[:], d["c_tri"])
            P.dma("sync", self.cmask[:], d["c_mask"])
            P.dma("gpsimd", self.cbf[:], d["c_bf"])
            P.dma("gpsimd", self.crow[:], d["c_row"])
            P.dma("sync", self.rope[:], d["c_rope"])
            P.dma("sync", self.lnp[:], d["lnp"])
            P.dma("sync", self.bup[:], d["bup"])
            P.dma("sync", self.gng[:], d["gng"])
            P.dma("gpsimd", self.wgk[:], d["wgk2aug"])
            P.dma("sync", self.sinkf[:], d["sinks"])
            skip = getattr(self, "skip", "").split(",")
            if "sink" not in skip:
                P.act(self.sinke[:], self.sinkf[:], AF.Exp)
                P.copy(self.sinkrow[:], self.sinke[:].unsqueeze(3).to_broadcast([2, NL, 4, 128]))
            if "memset" not in skip:
                P.memset(self.Vwin[:], 0.0)
                P.memset(self.bgkT[:], 1.0)
            self.identb = self.cbf[:, 0, :]
            self.ones = self.cbf[:, 1, :]
            self.ones_lo = self.cbf[:, 2, :]
            self.ones_hi = self.cbf[:, 3, :]
            self.onesm = self.cbf[:, 4, :]

            for b in range(self.n_seq):
                if "loadmem" not in skip:
                    self.load_mem(b)
                for sti in range(self.n_st):
                    ctx = dict(kind="p", b=b, st=sti, T=T, TA=128, NA=4)
                    if "loadx" not in skip:
                        self.load_x(ctx)
                    self.run_layers(ctx)
                    if "storey" not in skip:
                        self.store_y(ctx)
            if self.do_sample:
                ctx = dict(kind="s", b=0, st=0, T=TS, TA=TS, NA=1)
                self.load_x(ctx)
                self.run_layers(ctx)
                self.store_y(ctx)
            P.emit(st)
        return nc

    def arena_reset(self):
        self.acur = 0

    def af32(self, n):
        o = self.acur // 4
        self.acur += 4 * n
        assert self.acur <= 4 * ARENA, self.acur
        return self.arena[:, o:o + n]

    def abf(self, n):
        o = self.acur // 2
        self.acur += 2 * n
        self.acur = (self.acur + 3) // 4 * 4
        assert self.acur <= 4 * ARENA, self.acur
        return self.arena_bf[:, o:o + n]

    def pbank(self):
        return self.ps[:, self.P.bank(), :]

    def chunk(self, key, pieces):
        slot = self.ring_i % self.ring_n
        self.ring_i += 1
        ext = max(off + kc * n for (off, kc, n, _, _, _) in pieces)
        if key in self.wcache:
            ci = self.wcache[key]
            self.P.dma("gpsimd", self.ring[:, slot, 0:ext], self.wbf[ci, :, 0:ext], reads=["wbf%d" % ci])
        else:
            ci = len(self.wcache)
            self.wcache[key] = ci
            for (off, kc, n, src, p0, p1) in pieces:
                dst = self.ring[p0:p1, slot, off:off + kc * n].rearrange("p (k n) -> p k n", n=n)
                self.P.dma("gpsimd", dst, src)
            self.P.dma("sync", self.wbf[ci, :, 0:ext], self.ring[:, slot, 0:ext], writes=["wbf%d" % ci])
        return self.ring[:, slot, :]

    def wchunk(self, w, l, c0, n, r0=0, kc=8):
        src = w[l, r0:r0 + kc * 128, c0:c0 + n].rearrange("(k p) n -> p k n", p=128)
        v = self.chunk((w.tensor.name, l, c0, n, r0, kc), [(0, kc, n, src, 0, 128)])
        return v[:, 0:kc * n].rearrange("p (k n) -> p k n", n=n)

    def load_x(self, ctx):
        P, d = self.P, self.d
        TA, NA = ctx["TA"], ctx["NA"]
        for i in range(NA):
            slot = i % 2
            if ctx["kind"] == "p":
                t0 = ctx["st"] * T + i * TA
                src = d["xp"][ctx["b"], t0:t0 + TA, :]
            else:
                src = d["xs"][:, :]
            P.dma("sync", self.xstage[0:TA, slot, :], src)
            for hb in range(2):
                bk = self.pbank()
                for kk in range(4):
                    k = hb * 4 + kk
                    P.tr(bk[:, kk * 128:kk * 128 + TA], self.xstage[0:TA, slot, k * 128:(k + 1) * 128],
                         self.identf[0:TA, 0:TA])
                src_v = bk.rearrange("p (k t) -> p k t", t=128)[:, :, 0:TA]
                P.copy(self.xres[:, hb * 4:hb * 4 + 4, i * TA:(i + 1) * TA], src_v, eng="vector")
                P.copy(self.xb[:, hb * 4:hb * 4 + 4, i * TA:(i + 1) * TA], src_v, eng="scalar")

    def store_y(self, ctx):
        P, d = self.P, self.d
        TA, NA = ctx["TA"], ctx["NA"]
        for i in range(NA):
            slot = i % 2
            for hb in range(2):
                bk = self.pbank()
                for kk in range(4):
                    k = hb * 4 + kk
                    P.tr(bk[0:TA, kk * 128:(kk + 1) * 128], self.xres[:, k, i * TA:(i + 1) * TA], self.identf[:, :])
                if hb == 0:
                    P.copy(self.xstage[0:TA, slot, 0:512], bk[0:TA, :], eng="vector")
                else:
                    P.copy(self.xstage[0:TA, slot, 512:1024], bk[0:TA, :], eng="scalar")
            if ctx["kind"] == "p":
                t0 = ctx["st"] * T + i * TA
                dst = d["yp"][ctx["b"], t0:t0 + TA, :]
            else:
                dst = d["ys"][:, :]
            P.dma("sync", dst, self.xstage[0:TA, slot, :])

    def load_mem(self, b):
        P, d = self.P, self.d
        P.dma("sync", self.xstage[:, :, :], d["memp"][b].rearrange("(mt p) c -> p mt c", p=128))
        for mt in range(2):
            for hb in range(2):
                bk = self.pbank()
                for kk in range(4):
                    k = hb * 4 + kk
                    P.tr(bk[:, kk * 128:(kk + 1) * 128], self.xstage[:, mt, k * 128:(k + 1) * 128], self.identf[:, :])
                P.copy(self.memT[:, hb * 4:hb * 4 + 4, mt * 128:(mt + 1) * 128],
                       bk.rearrange("p (k t) -> p k t", t=128), eng=("vector" if hb == 0 else "scalar"))

    def run_layers(self, ctx):
        for l in range(self.n_layers):
            if l == 0:
                self.phase_mkv(l, ctx)
            self.phase_ab(l, ctx)
            self.phase_b2(l, ctx)
            self.phase_m(l, ctx)
            self.phase_merge(l, ctx)
            self.phase_wout(l, ctx)
            self.phase_mlp(l, ctx)
            if l + 1 < self.n_layers:
                self.phase_mkv(l + 1, ctx)
            self.arena_reset()
            self.ln_finish(l, 2, ctx)

    def phase_mkv(self, l, ctx):
        P, d = self.P, self.d
        if ctx["kind"] == "s":
            stg = self.xstage[:, 0, :].rearrange("p (mt c) -> p mt c", mt=2)
            P.dma("sync", stg, d["cmk"][l].rearrange("(mt p) c -> p mt c", p=128))
            for h in range(4):
                bk = self.pbank()
                for mt in range(2):
                    P.tr(bk[:, mt * 128:(mt + 1) * 128], stg[:, mt, h * 128:(h + 1) * 128], self.identf[:, :])
                P.copy(self.memkT[:, h, :], bk[:, 0:256], eng=("vector" if h % 2 == 0 else "scalar"))
            P.dma("gpsimd", self.memv[:, :, :], d["cmv"][l].rearrange("(mt p) c -> p mt c", p=128))
            return
        b, sti = ctx["b"], ctx["st"]
        cK = self.wchunk(d["w_mem_kv"], l, 0, 512)
        cV = self.wchunk(d["w_mem_kv"], l, 512, 512)
        first = (sti == 0)
        for mt in range(2):
            if first:
                bk = self.pbank()
                for k in range(8):
                    P.mm(bk, self.memT[:, k, mt * 128:(mt + 1) * 128], cK[:, k, :], start=(k == 0), stop=(k == 7))
                P.copy(self.ostage[:, 0, :], bk, eng="vector")
                P.dma("sync", d["mkp"][l, b, mt * 128:(mt + 1) * 128, :], self.ostage[:, 0, :])
            bk = self.pbank()
            for k in range(8):
                P.mm(bk, self.memT[:, k, mt * 128:(mt + 1) * 128], cV[:, k, :], start=(k == 0), stop=(k == 7))
            P.copy(self.memv[:, mt, :], bk, eng="scalar")
            if first:
                P.copy(self.ostage[:, 1, :], bk, eng="vector")
                P.dma("sync", d["mvp"][l, b, mt * 128:(mt + 1) * 128, :], self.ostage[:, 1, :])
        for h in range(4):
            bk = self.pbank()
            for k in range(8):
                P.mm(bk[:, 0:256], cK[:, k, h * 128:(h + 1) * 128], self.memT[:, k, :], start=(k == 0), stop=(k == 7))
            P.copy(self.memkT[:, h, :], bk[:, 0:256], eng=("vector" if h % 2 == 0 else "scalar"))

    def phase_m(self, l, ctx):
        P, d = self.P, self.d
        Tn = ctx["T"]
        cMq = self.wchunk(d["w_in"], l, C_MQ, 512)
        self.arena_reset()
        mqT = self.abf(4 * 512).rearrange("p (h t) -> p h t", h=4)
        PTs = [self.abf(2 * 512).rearrange("p (m t) -> p m t", m=2) for _ in range(2)]
        Rms = [self.af32(512) for _ in range(2)]
        for h in range(4):
            bk = self.pbank()
            for k in range(8):
                P.mm(bk[:, 0:Tn], cMq[:, k, h * 128:(h + 1) * 128], self.xb[:, k, 0:Tn], start=(k == 0), stop=(k == 7))
            P.copy(mqT[:, h, 0:Tn], bk[:, 0:Tn], eng=("vector" if h % 2 == 0 else "scalar"))
        def scores(h):
            PT = PTs[h % 2]
            for mt in range(2):
                bk = self.pbank()
                P.mm(bk[:, 0:Tn], self.memkT[:, h, mt * 128:(mt + 1) * 128], mqT[:, h, 0:Tn])
                P.act(PT[:, mt, 0:Tn], bk[:, 0:Tn], AF.Exp, scale=M_SCALE)

        scores(0)
        for h in range(4):
            PT = PTs[h % 2]
            Rm = Rms[h % 2]
            if h + 1 < 4:
                scores(h + 1)
            bD = self.pbank()
            P.mm(bD[:, 0:Tn], self.ones, PT[:, 0, 0:Tn], start=True, stop=False)
            P.mm(bD[:, 0:Tn], self.ones, PT[:, 1, 0:Tn], start=False, stop=True)
            bO = self.pbank()
            P.mm(bO[:, 0:Tn], self.memv[:, 0, h * 128:(h + 1) * 128], PT[:, 0, 0:Tn], start=True, stop=False)
            P.mm(bO[:, 0:Tn], self.memv[:, 1, h * 128:(h + 1) * 128], PT[:, 1, 0:Tn], start=False, stop=True)
            P.act(Rm[:, 0:Tn], bD[:, 0:Tn], AF.Ln)
            P.act(Rm[:, 0:Tn], Rm[:, 0:Tn], AF.Exp, scale=-1.0)
            P.tt(self.om[:, h, 0:Tn], bO[:, 0:Tn], Rm[:, 0:Tn], ALU.mult)

    def make_a(self, l, ctx):
        P, d = self.P, self.d
        TA, NA = ctx["TA"], ctx["NA"]
        samp = ctx["kind"] == "s"
        cAq = self.wchunk(d["w_in"], l, C_AQ, 512)
        cAkv = self.wchunk(d["w_in"], l, C_AK, 256)
        crow = self.crow
        sel2 = crow[0:2, 0:128]
        mpl = crow[0:1, 128:256]
        mcl = crow[0:1, 256:384]
        mpr = crow[0:1, 384:896]
        mcr = crow[0:1, 896:1408]
        if samp:
            stg = self.xstage[:, 1, 0:128]
            P.dma("sync", stg, d["cwk"][l])
            bk = self.pbank()
            P.tr(bk[:, 0:128], stg, self.identf[:, :])
            P.copy(self.KTwin[:, l, 0, :], bk[:, 0:128])
            P.dma("gpsimd", self.Vwin[:, l, 0, 0, 0:64], d["cwv"][l][:, 0:64])
            P.dma("gpsimd", self.Vwin[:, l, 0, 1, 64:128], d["cwv"][l][:, 64:128])
            P.dma("sync", d["wks"][l, 0:96, :], d["cwk"][l, 32:128, :])
            P.dma("sync", d["wvs"][l, 0:96, :], d["cwv"][l, 32:128, :])
        tiles = {}
        a_base = self.acur

        def stage1(i):
                self.acur = a_base
                if samp:
                    gi, slot, prev, has_prev, masked, rt = 1, 1, 0, True, False, 16
                else:
                    gi = ctx["st"] * 4 + i
                    slot, prev, has_prev, masked, rt = gi % 3, (gi - 1) % 3, gi > 0, True, gi
                last = samp or gi == 15
                qtok = self.abf(512).rearrange("p (c g e) -> p c g e", c=4, g=2)
                ktokb = self.abf(128)
                QT = self.abf(512).rearrange("p (c t) -> p c t", c=4)
                PT = [[self.abf(512) for _ in range(2)] for _ in range(2)]
                ktokf = self.af32(128)
                vtokf = self.af32(128)
                tq = [self.af32(64).rearrange("p (g c e) -> p g c e", g=2, c=4) for _ in range(4)]
                tk = [self.af32(16).rearrange("p (h e) -> p h e", h=2) for _ in range(4)]
                Ra = self.af32(512)
                bQ = self.pbank()
                bKV = self.pbank()
                xt = slice(i * TA, (i + 1) * TA)
                for k in range(8):
                    P.mm(bQ[0:TA, :], self.xb[:, k, xt], cAq[:, k, :], start=(k == 0), stop=(k == 7))
                for k in range(8):
                    P.mm(bKV[0:TA, 0:256], self.xb[:, k, xt], cAkv[:, k, :], start=(k == 0), stop=(k == 7))
                cos = self.rope[0:TA, rt, 0:8]
                sin = self.rope[0:TA, rt, 8:16]
                qv = bQ[0:TA, :].rearrange("p (g c e) -> p g c e", g=2, c=4)
                qo = qtok[0:TA].rearrange("p c g e -> p g c e")
                cq = cos.unsqueeze(1).unsqueeze(1).to_broadcast([TA, 2, 4, 8])
                sq_ = sin.unsqueeze(1).unsqueeze(1).to_broadcast([TA, 2, 4, 8])
                t0, t1, t2, t3 = [t[0:TA] for t in tq]
                P.tt(t0, qv[:, :, :, 0:8], cq, ALU.mult)
                P.tt(t1, qv[:, :, :, 8:16], sq_, ALU.mult)
                P.tt(qo[:, :, :, 0:8], t0, t1, ALU.subtract)
                P.tt(t2, qv[:, :, :, 8:16], cq, ALU.mult)
                P.tt(t3, qv[:, :, :, 0:8], sq_, ALU.mult)
                P.tt(qo[:, :, :, 8:16], t2, t3, ALU.add)
                P.copy(qo[:, :, :, 16:64], qv[:, :, :, 16:64], eng="scalar")
                kv = bKV[0:TA, 0:128].rearrange("p (h e) -> p h e", h=2)
                ko = ktokf[0:TA].rearrange("p (h e) -> p h e", h=2)
                ck = cos.unsqueeze(1).to_broadcast([TA, 2, 8])
                sk = sin.unsqueeze(1).to_broadcast([TA, 2, 8])
                u0, u1, u2, u3 = [t[0:TA] for t in tk]
                P.tt(u0, kv[:, :, 0:8], ck, ALU.mult)
                P.tt(u1, kv[:, :, 8:16], sk, ALU.mult)
                P.tt(ko[:, :, 0:8], u0, u1, ALU.subtract)
                P.tt(u2, kv[:, :, 8:16], ck, ALU.mult)
                P.tt(u3, kv[:, :, 0:8], sk, ALU.mult)
                P.tt(ko[:, :, 8:16], u2, u3, ALU.add)
                P.copy(ko[:, :, 16:64], kv[:, :, 16:64], eng="scalar")
                P.copy(ktokb[0:TA], ktokf[0:TA], eng="scalar")
                vdiag = self.Vwin[0:TA, l, slot].rearrange("p j (g e) -> p (j g) e", g=2)[:, 0::3, :]
                P.copy(vdiag, bKV[0:TA, 128:256].rearrange("p (g e) -> p g e", g=2), eng="vector")
                if last:
                    P.copy(vtokf[0:TA], bKV[0:TA, 128:256], eng="scalar")
                    if samp:
                        P.dma("sync", d["wks"][l, 96:128, :], ktokf[0:TA])
                        P.dma("sync", d["wvs"][l, 96:128, :], vtokf[0:TA])
                    else:
                        P.dma("sync", d["wkp"][l, ctx["b"]], ktokf[0:TA])
                        P.dma("sync", d["wvp"][l, ctx["b"]], vtokf[0:TA])
                tiles[i] = dict(QT=QT, PT=PT, Ra=Ra, slot=slot, prev=prev, has_prev=has_prev, masked=masked, xt=xt,
                                qtok=qtok, ktokb=ktokb)

        def stage1b(i):
                tl = tiles[i]
                qtok, ktokb, QT, slot = tl["qtok"], tl["ktokb"], tl["QT"], tl["slot"]
                bT = self.pbank().bitcast(BF16)
                for c in range(4):
                    P.tr(bT[:, c * 128:c * 128 + TA], qtok[0:TA, c].rearrange("p g e -> p (g e)"), self.identb[0:TA, 0:TA])
                P.tr(bT[:, 512:512 + TA], ktokb[0:TA], self.identb[0:TA, 0:TA])
                P.copy(QT[:, :, 0:TA], bT[:, 0:512].rearrange("p (c t) -> p c t", c=4)[:, :, 0:TA], eng="vector")
                P.copy(self.KTwin[:, l, slot, 0:TA], bT[:, 512:512 + TA], eng="scalar")

        def stage2(i):
            tl = tiles[i]
            QT, PT, Ra, slot, prev = tl["QT"], tl["PT"], tl["Ra"], tl["slot"], tl["prev"]
            has_prev, masked, xt = tl["has_prev"], tl["masked"], tl["xt"]
            NQ = 4 * TA
            for g in range(2):
                rows = slice(g * 64, (g + 1) * 64)
                qr = QT[rows, :, 0:TA]
                if has_prev:
                    bS = self.pbank()
                    P.mm(bS[:, 0:NQ], self.KTwin[rows, l, prev, :], qr, start=True, stop=(not masked))
                    if masked:
                        P.mm(bS[:, 0:NQ], mpl, mpr, start=False, stop=True)
                    P.act(PT[g][0][:, 0:NQ], bS[:, 0:NQ], AF.Exp, scale=A_SCALE)
                bS = self.pbank()
                P.mm(bS[0:TA, 0:NQ], self.KTwin[rows, l, slot, 0:TA], qr, start=True, stop=(not masked))
                if masked:
                    P.mm(bS[0:TA, 0:NQ], mcl, mcr, start=False, stop=True)
                P.act(PT[g][1][0:TA, 0:NQ], bS[0:TA, 0:NQ], AF.Exp, scale=A_SCALE)

        def stage2b(i):
            tl = tiles.pop(i)
            QT, PT, Ra, slot, prev = tl["QT"], tl["PT"], tl["Ra"], tl["slot"], tl["prev"]
            has_prev, masked, xt = tl["has_prev"], tl["masked"], tl["xt"]
            NQ = 4 * TA
            bD = self.pbank()
            first = True
            if has_prev:
                P.mm(bD[:, 0:NQ], self.ones_lo, PT[0][0][:, 0:NQ], start=True, stop=False)
                P.mm(bD[:, 0:NQ], self.ones_hi, PT[1][0][:, 0:NQ], start=False, stop=False)
                first = False
            P.mm(bD[:, 0:NQ], self.ones_lo[0:TA, :], PT[0][1][0:TA, 0:NQ], start=first, stop=False)
            P.mm(bD[:, 0:NQ], self.ones_hi[0:TA, :], PT[1][1][0:TA, 0:NQ], start=False, stop=False)
            P.mm(bD[:, 0:NQ], sel2, self.sinkrow[0:2, l, :, 0:TA], start=False, stop=True)
            bO = self.pbank()
            for c in range(4):
                cs = slice(c * TA, (c + 1) * TA)
                first = True
                if has_prev:
                    P.mm(bO[:, cs], self.Vwin[:, l, prev, 0, :], PT[0][0][:, cs], start=True, stop=False)
                    P.mm(bO[:, cs], self.Vwin[:, l, prev, 1, :], PT[1][0][:, cs], start=False, stop=False)
                    first = False
                P.mm(bO[:, cs], self.Vwin[0:TA, l, slot, 0, :], PT[0][1][0:TA, cs], start=first, stop=False)
                P.mm(bO[:, cs], self.Vwin[0:TA, l, slot, 1, :], PT[1][1][0:TA, cs], start=False, stop=True)
            P.act(Ra[:, 0:NQ], bD[:, 0:NQ], AF.Ln)
            P.act(Ra[:, 0:NQ], Ra[:, 0:NQ], AF.Exp, scale=-1.0)
            P.tt(self.oa[:, :, xt], bO[:, 0:NQ].rearrange("p (c t) -> p c t", c=4),
                 Ra[:, 0:NQ].rearrange("p (c t) -> p c t", c=4), ALU.mult)


        return stage1, stage1b, stage2, stage2b

    def phase_ab(self, l, ctx):
        P, d = self.P, self.d
        TA, NA, Tn = ctx["TA"], ctx["NA"], ctx["T"]
        samp = ctx["kind"] == "s"
        cBq = self.wchunk(d["w_in"], l, C_BQ, 512)
        cBk = self.wchunk(d["w_in"], l, C_BK, 512)
        cBv0 = self.wchunk(d["w_in"], l, C_BV, 512)
        cBv1 = self.wchunk(d["w_in"], l, C_BV + 512, 512)
        cBgk = self.wchunk(d["w_in"], l, C_BGK, 16)
        stl = self.state[:, l]
        if samp:
            P.dma("sync", stl, d["sg"][l].rearrange("h k v -> k h v"))
        elif ctx["st"] == 0:
            P.memset(stl, 0.0)
        P.copy(self.state_bf[:], stl, eng="scalar")
        self.arena_reset()
        sp = self.af32(512)
        Eq = self.af32(512).rearrange("p (h t) -> p h t", h=4)
        Ek = self.af32(512).rearrange("p (h t) -> p h t", h=4)
        Er = self.af32(512)
        e_t = Er
        srt = self.af32(512).rearrange("p (h t) -> p h t", h=4)
        rp = srt
        attT = self.abf(512).rearrange("p (h t) -> p h t", h=4)
        sq = self.abf(1024).rearrange("p (c t) -> p c t", c=8)
        cross = []
        for _ in range(1):
            cross.append(dict(qdT=self.abf(512).rearrange("p (h t) -> p h t", h=4),
                              kiT=self.abf(512).rearrange("p (h t) -> p h t", h=4),
                              kend=self.abf(512), vtok=self.abf(1024), elast=self.af32(4)))
        obraw = self.xstage[:, 0, :].rearrange("p (c t) -> p c t", c=8)
        qraw, kraw = self.mg[:, 0:4, :], self.mg[:, 4:8, :]
        a_s1a, a_s1b, a_s2a, a_s2b = self.make_a(l, ctx)
        b0 = self.pbank()
        for j, (cw, dst) in enumerate(((cBq, qraw), (cBk, kraw))):
            bks = [self.pbank() for _ in range(4)]
            for k in range(8):
                if j == 0:
                    P.mm(b0[0:16, 0:Tn], cBgk[:, k, 0:16], self.xb[:, k, 0:Tn], start=(k == 0), stop=(k == 7))
                for h in range(4):
                    P.mm(bks[h][:, 0:Tn], cw[:, k, h * 128:(h + 1) * 128], self.xb[:, k, 0:Tn], start=(k == 0), stop=(k == 7))
            if j == 0:
                P.copy(self.bgkT[0:16, 0:Tn], b0[0:16, 0:Tn], eng="vector")
            for h in range(4):
                P.copy(dst[:, h, 0:Tn], bks[h][:, 0:Tn], eng=("vector" if (h + j) % 2 == 0 else "scalar"))

        st1 = {}

        def stage1z(i):
            xt = slice(i * TA, (i + 1) * TA)
            bZ = self.pbank()
            P.mm(bZ[0:TA, :], self.bgkT[0:17, xt], self.wgk[0:17, l, :])
            P.act(e_t[0:TA], bZ[0:TA, :], AF.Exp, scale=-1.0)
            P.act(sp[0:TA], e_t[0:TA], AF.Ln, bias=1.0)

        def stage1(i):
            c = cross[0]
            qdT, kiT, kend, vtok, elast = c["qdT"], c["kiT"], c["kend"], c["vtok"], c["elast"]
            xt = slice(i * TA, (i + 1) * TA)
            for hv, cw in enumerate((cBv0, cBv1)):
                bV = self.pbank()
                for k in range(8):
                    P.mm(bV[0:TA, :], self.xb[:, k, xt], cw[:, k, :], start=(k == 0), stop=(k == 7))
                P.copy(vtok[0:TA, hv * 512:(hv + 1) * 512], bV[0:TA, :], eng=("vector" if hv == 0 else "scalar"))

        def stage1b(i):
            c = cross[0]
            qdT, kiT, kend, vtok, elast = c["qdT"], c["kiT"], c["kend"], c["vtok"], c["elast"]
            xt = slice(i * TA, (i + 1) * TA)
            bC = self.pbank()
            for h in range(4):
                P.mm(bC[:, h * 128:h * 128 + TA], sp[0:TA, h * 128:(h + 1) * 128], self.tri[0:TA, 0, 0:TA])
            bR = self.pbank()
            P.mm(bR[0:TA, :], self.tri[0:TA, 1, 0:TA], sp[0:TA, :])
            bKt = self.pbank()
            for k in range(8):
                P.mm(bKt[0:TA, :], self.xb[:, k, xt], cBk[:, k, :], start=(k == 0), stop=(k == 7))
            bCv = bC.rearrange("p (h t) -> p h t", h=4)[:, :, 0:TA]
            P.act(Eq[:, :, 0:TA], bCv, AF.Exp)
            P.act(Ek[:, :, 0:TA], bCv, AF.Exp, scale=-1.0)
            P.act(Er[0:TA], bR[0:TA, :], AF.Exp)
            P.copy(elast[:, 0:4], Eq[:, :, TA - 1], eng="scalar")
            P.tt(kend[0:TA], bKt[0:TA, :], Er[0:TA], ALU.mult)
            P.stt(qdT[:, :, 0:TA], qraw[:, :, xt], BQ_SCALE, Eq[:, :, 0:TA], ALU.mult, ALU.mult)
            P.tt(kiT[:, :, 0:TA], kraw[:, :, xt], Ek[:, :, 0:TA], ALU.mult)

        def stage2(i):
            c = cross[0]
            qdT, kiT, kend, vtok, elast = c["qdT"], c["kiT"], c["kend"], c["vtok"], c["elast"]
            xt = slice(i * TA, (i + 1) * TA)
            bA = self.pbank()
            for h in range(4):
                P.mm(bA[0:TA, h * 128:h * 128 + TA], kiT[:, h, 0:TA], qdT[:, h, 0:TA])
            P.tt(attT[0:TA, :, 0:TA], bA[0:TA].rearrange("p (h t) -> p h t", h=4)[:, :, 0:TA],
                 self.cmask[0:TA, 0:TA].unsqueeze(1).to_broadcast([TA, 4, TA]), ALU.mult)

        def stage2b(i):
            c = cross[0]
            qdT, kiT, kend, vtok, elast = c["qdT"], c["kiT"], c["kend"], c["vtok"], c["elast"]
            xt = slice(i * TA, (i + 1) * TA)
            bOs = [self.pbank(), self.pbank()]
            for h in range(4):
                for hf in range(2):
                    reg = bOs[h // 2][:, ((h % 2) * 2 + hf) * 128:((h % 2) * 2 + hf) * 128 + TA]
                    P.mm(reg, vtok[0:TA, h * 256 + hf * 128:h * 256 + (hf + 1) * 128], attT[0:TA, h, 0:TA],
                         start=True, stop=False)
                    P.mm(reg, self.state_bf[:, h, hf * 128:(hf + 1) * 128], qdT[:, h, 0:TA], start=False, stop=True)
            for j in range(2):
                ov = bOs[j].rearrange("p (c t) -> p c t", c=4)[:, :, 0:TA]
                P.act(obraw[:, j * 4:(j + 1) * 4, 0:TA], ov, AF.Copy)
                P.act(sq[:, j * 4:(j + 1) * 4, 0:TA], ov, AF.Square)
            bDs = [self.pbank(), self.pbank()]
            for h in range(4):
                reg = bDs[h // 2][:, (h % 2) * 256:(h % 2 + 1) * 256]
                P.mm(reg, kend[0:TA, h * 128:(h + 1) * 128], vtok[0:TA, h * 256:(h + 1) * 256])
            for h in range(4):
                reg = bDs[h // 2][:, (h % 2) * 256:(h % 2 + 1) * 256]
                P.stt(stl[:, h, :], stl[:, h, :], elast[:, h:h + 1], reg, ALU.mult, ALU.add)
            if i + 1 < NA:
                P.copy(self.state_bf[:], stl, eng="scalar")

        def stage2c(i):
            xt = slice(i * TA, (i + 1) * TA)
            bS = self.pbank()
            for h in range(4):
                P.mm(bS[:, h * 128:h * 128 + TA], self.ones, sq[:, 2 * h, 0:TA], start=True, stop=False)
                P.mm(bS[:, h * 128:h * 128 + TA], self.ones, sq[:, 2 * h + 1, 0:TA], start=False, stop=True)
            P.act(srt[:, :, 0:TA], bS.rearrange("p (h t) -> p h t", h=4)[:, :, 0:TA], AF.Ln,
                  bias=RMS_EPS, scale=1.0 / 256.0)
            P.act(rp[:, :, 0:TA], srt[:, :, 0:TA], AF.Exp, scale=-0.5)
            for h in range(4):
                P.tt(self.ob[:, 2 * h:2 * h + 2, xt], obraw[:, 2 * h:2 * h + 2, 0:TA],
                     rp[:, h, 0:TA].unsqueeze(1).to_broadcast([128, 2, TA]), ALU.mult)

        a_s1a(0)
        for i in range(NA):
            stage1z(i)
            if i > 0:
                stage2b(i - 1)
            stage1(i)
            a_s1b(i)
            if i > 0:
                stage2c(i - 1)
            stage1b(i)
            a_s2a(i)
            stage2(i)
            if i + 1 < NA:
                a_s1a(i + 1)
            a_s2b(i)
        stage2b(NA - 1)
        stage2c(NA - 1)
        if samp:
            P.dma("sync", d["gss"][l].rearrange("h k v -> k h v"), stl)
        elif ctx["st"] == 3:
            P.dma("sync", d["gsp"][l, ctx["b"]].rearrange("h k v -> k h v"), stl)

    def phase_b2(self, l, ctx):
        P, d = self.P, self.d
        Tn = ctx["T"]
        cBg = [self.wchunk(d["w_in"], l, C_BG, 512), self.wchunk(d["w_in"], l, C_BG + 512, 512)]
        self.arena_reset()
        gss = [self.af32(512) for _ in range(2)]
        for c in range(8):
            bk = self.pbank()
            for k in range(8):
                P.mm(bk[:, 0:Tn], cBg[c // 4][:, k, (c % 4) * 128:(c % 4 + 1) * 128], self.xb[:, k, 0:Tn],
                     start=(k == 0), stop=(k == 7))
            gs = gss[c % 2]
            P.act(gs[:, 0:Tn], bk[:, 0:Tn], AF.Silu)
            P.stt(self.ob[:, c, 0:Tn], gs[:, 0:Tn], self.gng[:, l, c % 2:c % 2 + 1], self.ob[:, c, 0:Tn],
                  ALU.mult, ALU.mult)

    def phase_merge(self, l, ctx):
        P, d = self.P, self.d
        Tn = ctx["T"]
        self.arena_reset()
        sgs = [[self.af32(512) for _ in range(3)] for _ in range(2)]
        if self.dbg and ctx.get("b", 0) == 0 and ctx["st"] == 0:
            tg = "%s%d" % (ctx["kind"], l)
            self.dbg_dump_bf("dbg_oa_" + tg, self.oa, 4, Tn)
            self.dbg_dump_bf("dbg_ob_" + tg, self.ob, 8, Tn)
            self.dbg_dump_bf("dbg_om_" + tg, self.om, 4, Tn)
        for f in range(8):
            fs = slice(f * 128, (f + 1) * 128)
            pieces = []
            for g in range(2):
                src = d["w_proj_a"][l, g * 256:(g + 1) * 256, fs].rearrange("(c e) n -> e c n", e=64)
                pieces.append((0, 4, 128, src, g * 64, (g + 1) * 64))
            pieces.append((512, 8, 128, d["w_proj_b"][l, :, fs].rearrange("(k p) n -> p k n", p=128), 0, 128))
            pieces.append((1536, 4, 128, d["w_proj_m"][l, :, fs].rearrange("(k p) n -> p k n", p=128), 0, 128))
            cP = self.chunk(("P", l, f), pieces)
            pa = cP[:, 0:512].rearrange("p (k n) -> p k n", n=128)
            pb = cP[:, 512:1536].rearrange("p (k n) -> p k n", n=128)
            pm = cP[:, 1536:2048].rearrange("p (k n) -> p k n", n=128)
            pieces = []
            for j in range(3):
                c0 = C_G + j * 1024 + f * 128
                pieces.append((j * 1024, 8, 128, d["w_in"][l, :, c0:c0 + 128].rearrange("(k p) n -> p k n", p=128), 0, 128))
            cG = self.chunk(("G", l, f), pieces)
            gw = [cG[:, j * 1024:(j + 1) * 1024].rearrange("p (k n) -> p k n", n=128) for j in range(3)]
            sg = sgs[f % 2]
            bG = []
            for j in range(3):
                bk = self.pbank()
                for k in range(8):
                    P.mm(bk[:, 0:Tn], gw[j][:, k, :], self.xb[:, k, 0:Tn], start=(k == 0), stop=(k == 7))
                P.act(sg[j][:, 0:Tn], bk[:, 0:Tn], AF.Sigmoid)
            bPA = self.pbank()
            for c in range(4):
                P.mm(bPA[:, 0:Tn], pa[:, c, :], self.oa[:, c, 0:Tn], start=(c == 0), stop=(c == 3))
            P.tt(sg[0][:, 0:Tn], bPA[:, 0:Tn], sg[0][:, 0:Tn], ALU.mult)
            bPB = self.pbank()
            for c in range(8):
                P.mm(bPB[:, 0:Tn], pb[:, c, :], self.ob[:, c, 0:Tn], start=(c == 0), stop=(c == 7))
            P.tt(sg[1][:, 0:Tn], bPB[:, 0:Tn], sg[1][:, 0:Tn], ALU.mult)
            bPM = self.pbank()
            for c in range(4):
                P.mm(bPM[:, 0:Tn], pm[:, c, :], self.om[:, c, 0:Tn], start=(c == 0), stop=(c == 3))
            P.tt(sg[2][:, 0:Tn], bPM[:, 0:Tn], sg[2][:, 0:Tn], ALU.mult)
            P.tt(sg[0][:, 0:Tn], sg[0][:, 0:Tn], sg[1][:, 0:Tn], ALU.add)
            P.tt(self.mg[:, f, 0:Tn], sg[0][:, 0:Tn], sg[2][:, 0:Tn], ALU.add)
        if self.dbg and ctx.get("b", 0) == 0 and ctx["st"] == 0:
            self.dbg_dump_bf("dbg_mg_%s%d" % (ctx["kind"], l), self.mg, 8, Tn)

    def ln_prep(self, ctx):
        self.ln_ysq = [self.abf(512) for _ in range(2)]
        self.ln_bM = self.ps[:, 6, :]
        self.ln_bQ = self.ps[:, 7, :]

    def ln_act(self, f, ctx):
        P, Tn = self.P, ctx["T"]
        P.copy(self.xb[:, f, 0:Tn], self.xres[:, f, 0:Tn], eng="scalar")
        P.act(self.ln_ysq[f % 2][:, 0:Tn], self.xres[:, f, 0:Tn], AF.Square)

    def ln_mm(self, f, ctx):
        P, Tn = self.P, ctx["T"]
        P.mm(self.ln_bM[:, 0:Tn], self.onesm, self.xb[:, f, 0:Tn], start=(f == 0), stop=(f == 7))
        P.mm(self.ln_bQ[:, 0:Tn], self.onesm, self.ln_ysq[f % 2][:, 0:Tn], start=(f == 0), stop=(f == 7))

    def ln_finish(self, l, which, ctx):
        P = self.P
        Tn = ctx["T"]
        gi, bi = (0, 1) if which == 1 else (2, 3)
        bM, bQ = self.ln_bM, self.ln_bQ
        self.acur = 2048
        msb = self.af32(512)
        var = self.af32(512)
        rstd = self.af32(512)
        tmp = [self.af32(512) for _ in range(3)]
        P.copy(msb[:, 0:Tn], bM[:, 0:Tn], eng="scalar")
        P.act(var[:, 0:Tn], bM[:, 0:Tn], AF.Square)
        P.tt(var[:, 0:Tn], bQ[:, 0:Tn], var[:, 0:Tn], ALU.subtract)
        P.act(rstd[:, 0:Tn], var[:, 0:Tn], AF.Ln, bias=LN_EPS)
        P.act(rstd[:, 0:Tn], rstd[:, 0:Tn], AF.Exp, scale=-0.5)
        tmp = tmp + [self.af32(512) for _ in range(2)]

        def res_affine(f):
            P.act(self.xres[:, f, 0:Tn], tmp[f % 5][:, 0:Tn], AF.Identity, bias=self.lnp[:, l, bi, f:f + 1],
                  scale=self.lnp[:, l, gi, f:f + 1])

        for f in range(8):
            t = tmp[f % 5]
            P.tt(t[:, 0:Tn], self.xres[:, f, 0:Tn], msb[:, 0:Tn], ALU.subtract)
            P.tt(t[:, 0:Tn], t[:, 0:Tn], rstd[:, 0:Tn], ALU.mult)
            P.act(self.xb[:, f, 0:Tn], t[:, 0:Tn], AF.Identity, bias=self.lnp[:, l, bi, f:f + 1],
                  scale=self.lnp[:, l, gi, f:f + 1])
            if f >= 3:
                res_affine(f - 3)
        for f in range(5, 8):
            res_affine(f)

    def phase_wout(self, l, ctx):
        P, d = self.P, self.d
        Tn = ctx["T"]
        cWo = [self.wchunk(d["w_out"], l, 0, 1024, r0=0, kc=4), self.wchunk(d["w_out"], l, 0, 1024, r0=512, kc=4)]
        self.arena_reset()
        self.ln_prep(ctx)
        for f in range(8):
            bk = self.pbank()
            for k in range(8):
                P.mm(bk[:, 0:Tn], cWo[k // 4][:, k % 4, f * 128:(f + 1) * 128], self.mg[:, k, 0:Tn],
                     start=(k == 0), stop=(k == 7))
            P.stt(self.xres[:, f, 0:Tn], self.xres[:, f, 0:Tn], ALPHA, bk[:, 0:Tn], ALU.mult, ALU.add)
            self.ln_act(f, ctx)
            if f > 0:
                self.ln_mm(f - 1, ctx)
        self.ln_mm(7, ctx)
        self.ln_finish(l, 1, ctx)
        if self.dbg and ctx.get("b", 0) == 0 and ctx["st"] == 0:
            self.dbg_dump("dbg_x1_%s%d" % (ctx["kind"], l), self.xres[:, :, 0:Tn], (128, 8, Tn))

    def phase_mlp(self, l, ctx):
        P, d = self.P, self.d
        Tn = ctx["T"]
        h1 = self.ob
        for f in range(8):
            P.ts(self.xres[:, f, 0:Tn], self.xres[:, f, 0:Tn], ALPHA, ALU.mult, self.lnp[:, l, 4, f:f + 1], ALU.add)
        self.arena_reset()
        self.ln_prep(ctx)
        rt = [self.abf(512) for _ in range(2)]
        for g in range(4):
            cU = [self.wchunk(d["w_up"], l, g * 1024, 512), self.wchunk(d["w_up"], l, g * 1024 + 512, 512)]
            pre = {}
            if g == 0:
                for fc in range(4):
                    pre[fc] = self.pbank()
                for k in range(8):
                    for fc in range(4):
                        P.mm(pre[fc][:, 0:Tn], cU[0][:, k, fc * 128:(fc + 1) * 128], self.xb[:, k, 0:Tn],
                             start=(k == 0), stop=(k == 7))
            for fc in range(8):
                if fc in pre:
                    bk = pre[fc]
                else:
                    bk = self.pbank()
                    for k in range(8):
                        P.mm(bk[:, 0:Tn], cU[fc // 4][:, k, (fc % 4) * 128:(fc % 4 + 1) * 128], self.xb[:, k, 0:Tn],
                             start=(k == 0), stop=(k == 7))
                r = rt[fc % 2]
                P.act(r[:, 0:Tn], bk[:, 0:Tn], AF.Relu, bias=self.bup[:, l, g * 8 + fc:g * 8 + fc + 1])
                P.tt(h1[:, fc, 0:Tn], r[:, 0:Tn], r[:, 0:Tn], ALU.mult)
            cD = [self.wchunk(d["w_down"], l, 0, 1024, r0=g * 1024, kc=4),
                  self.wchunk(d["w_down"], l, 0, 1024, r0=g * 1024 + 512, kc=4)]
            for f in range(8):
                bk = self.pbank()
                for k in range(8):
                    P.mm(bk[:, 0:Tn], cD[k // 4][:, k % 4, f * 128:(f + 1) * 128], h1[:, k, 0:Tn],
                         start=(k == 0), stop=(k == 7))
                P.tt(self.xres[:, f, 0:Tn], self.xres[:, f, 0:Tn], bk[:, 0:Tn], ALU.add)
                if g == 3:
                    self.ln_act(f, ctx)
                    if f > 0:
                        self.ln_mm(f - 1, ctx)
        self.ln_mm(7, ctx)


def _small_params(inp):
    f = np.float32
    lnp = np.zeros((128, NL, 5, 8), f)
    for j, nm in enumerate(("ln1_g", "ln1_b", "ln2_g", "ln2_b", "b_down")):
        lnp[:, :, j, :] = np.asarray(inp[nm], f).reshape(NL, 8, 128).transpose(2, 0, 1)
    bup = np.ascontiguousarray(np.asarray(inp["b_up"], f).reshape(NL, 32, 128).transpose(2, 0, 1))
    gng = np.ascontiguousarray(np.asarray(inp["gla_norm_g"], f).reshape(NL, 2, 128).transpose(2, 0, 1))
    wgk = np.zeros((17, NL, 512), f)
    wgk[0:16] = np.asarray(inp["w_gk2"], f).transpose(1, 0, 2)
    wgk[16] = np.asarray(inp["b_gk"], f)
    sinks = np.ascontiguousarray(np.asarray(inp["attn_sinks"], f).reshape(NL, 2, 4).transpose(1, 0, 2))
    return dict(lnp=lnp, bup=bup, gng=gng, wgk2aug=wgk, sinks=sinks)


def make_in_maps(inp, n_cores=8, n_seq=4):
    f = np.float32
    shared = {k: np.ascontiguousarray(np.asarray(inp[k], f)) for k in
              ("w_in", "w_mem_kv", "w_proj_a", "w_proj_b", "w_proj_m", "w_out", "w_up", "w_down")}
    shared.update(_consts())
    shared.update(_small_params(inp))
    maps = []
    for c in range(n_cores):
        m = dict(shared)
        m["xp"] = np.ascontiguousarray(np.asarray(inp["x_prompt"][c * n_seq:(c + 1) * n_seq], f))
        m["memp"] = np.ascontiguousarray(np.asarray(inp["mem_prompt"][c * n_seq:(c + 1) * n_seq], f))
        m["xs"] = np.ascontiguousarray(np.asarray(inp["x_sample"][c], f))
        m["cwk"] = np.ascontiguousarray(np.asarray(inp["cache_win_k"][:, c], f).reshape(NL, 128, 128))
        m["cwv"] = np.ascontiguousarray(np.asarray(inp["cache_win_v"][:, c], f).reshape(NL, 128, 128))
        m["sg"] = np.ascontiguousarray(np.asarray(inp["state_gla"][:, c], f))
        m["cmk"] = np.ascontiguousarray(np.asarray(inp["cache_mem_k"][:, c], f).reshape(NL, 256, 512))
        m["cmv"] = np.ascontiguousarray(np.asarray(inp["cache_mem_v"][:, c], f).reshape(NL, 256, 512))
        maps.append(m)
    return maps


def kernel(**inp):
    bld = Builder(n_seq=4, n_layers=NL, do_sample=True, n_st=4)
    nc = bld.build()
    maps = make_in_maps(inp, 8, 4)
    res = run_bass_kernel_spmd(nc, maps, core_ids=list(range(8)))
    R = res.results
    f = np.float32
    y_p = np.concatenate([r["yp"] for r in R], axis=0).astype(f)
    y_s = np.stack([r["ys"] for r in R], axis=0).astype(f)
    wkp = np.concatenate([r["wkp"] for r in R], axis=1).reshape(NL, NB, 128, 2, 64).astype(f)
    wvp = np.concatenate([r["wvp"] for r in R], axis=1).reshape(NL, NB, 128, 2, 64).astype(f)
    gsp = np.concatenate([r["gsp"] for r in R], axis=1).astype(f)
    mkp = np.concatenate([r["mkp"] for r in R], axis=1).reshape(NL, NB, 256, 4, 128).astype(f)
    mvp = np.concatenate([r["mvp"] for r in R], axis=1).reshape(NL, NB, 256, 4, 128).astype(f)
    wks = np.stack([r["wks"] for r in R], axis=1).reshape(NL, 8, 128, 2, 64).astype(f)
    wvs = np.stack([r["wvs"] for r in R], axis=1).reshape(NL, 8, 128, 2, 64).astype(f)
    gss = np.stack([r["gss"] for r in R], axis=1).astype(f)
    return (y_p, y_s, wkp, wvp, gsp, mkp, mvp, wks, wvs, gss)
```
